# Optimizing a Trainium2 kernel written in Bass

```python
import jax, jax.numpy as jnp
from jax import lax
import numpy as np

D_MODEL = 2048
BATCH = 2
SEQ = 16384
DEPTH = 1

MIX_WIDTH = D_MODEL
SB_HEADS = 8
SB_HEAD_DIM = 128
SB_WIDTH = SB_HEADS * SB_HEAD_DIM
SB_BLOCK = 128
SSD_HEADS = 16
SSD_HEAD_DIM = 64
SSD_WIDTH = SSD_HEADS * SSD_HEAD_DIM
SSD_GROUPS = 2
SSD_STATE = 128
SSD_CONV = 4
SSD_CHUNK = 128
SSD_CONV_DIM = SSD_WIDTH + 2 * SSD_GROUPS * SSD_STATE
IN_PROJ_DIM = 3 * SB_WIDTH + SSD_WIDTH + SSD_CONV_DIM + SSD_HEADS
PEER_HEADS = 8
PEER_N_KEYS = 128
PEER_N_EXPERTS = PEER_N_KEYS * PEER_N_KEYS
PEER_KEY_DIM = 256
PEER_HALF = PEER_KEY_DIM // 2
PEER_TOPK = 16
PEER_TOKEN_BLOCK = 128
RMS_EPS = 1e-6

kernel_name = "hymba_sb_ssd_peer_block"


def rms_norm(x, gain):
    xf = x.astype(jnp.float32)
    inv = lax.rsqrt(jnp.mean(xf * xf, axis=-1, keepdims=True) + RMS_EPS)
    return (xf * inv).astype(x.dtype) * gain


def stick_breaking_attention(q, k, v):
    b, s, h, dh = q.shape
    nb = s // SB_BLOCK
    scale = dh ** -0.5
    kf = k.astype(jnp.float32)
    vf = v.astype(jnp.float32)
    key_pos = jnp.arange(s)
    q_blocks = q.reshape(b, nb, SB_BLOCK, h, dh).transpose(1, 0, 2, 3, 4)

    def block(args):
        qb, bi = args
        q_pos = bi * SB_BLOCK + jnp.arange(SB_BLOCK)
        logits = jnp.einsum('bqhd,bkhd->bhqk', qb.astype(jnp.float32), kf) * scale
        causal = key_pos[None, :] < q_pos[:, None]
        log_beta = jax.nn.log_sigmoid(logits)
        log_keep = jnp.where(causal, jax.nn.log_sigmoid(-logits), 0.0)
        later = lax.cumsum(log_keep, axis=3, reverse=True) - log_keep
        weights = jnp.where(causal, jnp.exp(jnp.where(causal, log_beta + later, 0.0)), 0.0)
        return jnp.einsum('bhqk,bkhd->bqhd', weights, vf)

    out = lax.map(block, (q_blocks, jnp.arange(nb)))
    return out.transpose(1, 0, 2, 3, 4).reshape(b, s, h, dh).astype(q.dtype)


def ssd_scan(x, dt, a, bmat, cmat):
    f32 = jnp.float32
    b, s, h, p = x.shape
    g, n = bmat.shape[-2], bmat.shape[-1]
    hg = h // g
    nc, l = s // SSD_CHUNK, SSD_CHUNK
    xf = (x.astype(f32) * dt[..., None]).reshape(b, nc, l, g, hg, p)
    a_dt = (dt * a).reshape(b, nc, l, g, hg).transpose(0, 3, 4, 1, 2)
    bf = bmat.astype(f32).reshape(b, nc, l, g, n)
    cf = cmat.astype(f32).reshape(b, nc, l, g, n)
    a_cs = jnp.cumsum(a_dt, axis=-1)
    seg = a_cs[..., :, None] - a_cs[..., None, :]
    tri = jnp.tril(jnp.ones((l, l), dtype=bool))
    decay = jnp.where(tri, jnp.exp(jnp.where(tri, seg, 0.0)), 0.0).transpose(0, 3, 1, 2, 4, 5)
    cb = jnp.einsum('bclgn,bcsgn->bcgls', cf, bf)
    scores = cb[:, :, :, None] * decay
    y_diag = jnp.einsum('bcghls,bcsghp->bclghp', scores, xf)
    decay_states = jnp.exp(a_cs[..., -1:] - a_cs).transpose(0, 3, 4, 1, 2)
    states = jnp.einsum('bclgn,bclghp->bcghpn', bf, xf * decay_states[..., None])
    chunk_decay = jnp.exp(a_cs[..., -1]).transpose(3, 0, 1, 2)

    def step(carry, inp):
        st, dec = inp
        return carry * dec[..., None, None] + st, carry

    init = jnp.zeros((b, g, hg, p, n), f32)
    _, prev = lax.scan(step, init, (states.transpose(1, 0, 2, 3, 4, 5), chunk_decay))
    prev = prev.transpose(1, 0, 2, 3, 4, 5)
    state_decay_out = jnp.exp(a_cs).transpose(0, 3, 4, 1, 2)
    y_off = jnp.einsum('bclgn,bcghpn->bclghp', cf, prev) * state_decay_out[..., None]
    return (y_diag + y_off).reshape(b, s, h, p)


def hybrid_mixer(xn, w_in, sb_q_norm, sb_k_norm, conv_w, conv_b, dt_bias, a_log, d_skip,
                 sb_out_norm, ssd_out_norm, w_out):
    b, s, _ = xn.shape
    proj = xn @ w_in
    o1 = SB_WIDTH
    o2 = 2 * SB_WIDTH
    o3 = 3 * SB_WIDTH
    o4 = o3 + SSD_WIDTH
    o5 = o4 + SSD_CONV_DIM
    q, k, v, z, xbc, dt_raw = jnp.split(proj, [o1, o2, o3, o4, o5], axis=-1)
    q = rms_norm(q.reshape(b, s, SB_HEADS, SB_HEAD_DIM), sb_q_norm)
    k = rms_norm(k.reshape(b, s, SB_HEADS, SB_HEAD_DIM), sb_k_norm)
    v = v.reshape(b, s, SB_HEADS, SB_HEAD_DIM)
    attn = stick_breaking_attention(q, k, v).reshape(b, s, SB_WIDTH)
    attn = rms_norm(attn, sb_out_norm)
    xbc = lax.conv_general_dilated(xbc, conv_w, window_strides=(1,), padding=[(SSD_CONV - 1, 0)],
                                   dimension_numbers=('NWC', 'WIO', 'NWC'),
                                   feature_group_count=SSD_CONV_DIM) + conv_b
    xbc = jax.nn.silu(xbc)
    xs, bm, cm = jnp.split(xbc, [SSD_WIDTH, SSD_WIDTH + SSD_GROUPS * SSD_STATE], axis=-1)
    xs = xs.reshape(b, s, SSD_HEADS, SSD_HEAD_DIM)
    bm = bm.reshape(b, s, SSD_GROUPS, SSD_STATE)
    cm = cm.reshape(b, s, SSD_GROUPS, SSD_STATE)
    dt = jax.nn.softplus(dt_raw.astype(jnp.float32) + dt_bias.astype(jnp.float32))
    a = -jnp.exp(a_log.astype(jnp.float32))
    y = ssd_scan(xs, dt, a, bm, cm) + d_skip[:, None] * xs
    y = y.reshape(b, s, SSD_WIDTH).astype(xn.dtype) * jax.nn.silu(z)
    y = rms_norm(y.reshape(b, s, SSD_GROUPS, SSD_WIDTH // SSD_GROUPS),
                 ssd_out_norm.reshape(SSD_GROUPS, SSD_WIDTH // SSD_GROUPS)).reshape(b, s, SSD_WIDTH)
    return jnp.concatenate([attn, y], axis=-1) @ w_out


def peer_ffn(hn, w_query, sub_keys, expert_down, expert_up):
    b, s, d = hn.shape
    t = b * s
    xt = hn.reshape(t, d)
    q = (xt @ w_query).reshape(t, PEER_HEADS, 2, PEER_HALF)
    scores = jnp.einsum('thcd,hckd->thck', q.astype(jnp.float32), sub_keys.astype(jnp.float32))
    top_s, top_i = lax.top_k(scores, PEER_TOPK)
    cand_s = (top_s[:, :, 0, :, None] + top_s[:, :, 1, None, :]).reshape(t, PEER_HEADS, PEER_TOPK * PEER_TOPK)
    cand_i = (top_i[:, :, 0, :, None] * PEER_N_KEYS + top_i[:, :, 1, None, :]).reshape(t, PEER_HEADS, PEER_TOPK * PEER_TOPK)
    best_s, best_pos = lax.top_k(cand_s, PEER_TOPK)
    expert_idx = jnp.take_along_axis(cand_i, best_pos, axis=-1)
    gates = jax.nn.softmax(best_s, axis=-1)
    nblk = t // PEER_TOKEN_BLOCK

    def block(args):
        xb, idx, gb = args
        u = expert_down[idx]
        act = jax.nn.gelu(jnp.einsum('thkd,td->thk', u, xb), approximate=False)
        vv = expert_up[idx]
        return jnp.einsum('thk,thkd->td', (gb * act).astype(vv.dtype), vv)

    out = lax.map(block, (xt.reshape(nblk, PEER_TOKEN_BLOCK, d),
                          expert_idx.reshape(nblk, PEER_TOKEN_BLOCK, PEER_HEADS, PEER_TOPK),
                          gates.reshape(nblk, PEER_TOKEN_BLOCK, PEER_HEADS, PEER_TOPK)))
    return out.reshape(b, s, d).astype(hn.dtype)


def setup_inputs(seed: int = 0) -> dict:
    key = jax.random.key(seed)
    ks = jax.random.split(key, 20)
    f32 = jnp.float32
    nrm = lambda k, shape, sc: jax.random.normal(k, shape, f32) * sc
    gain = lambda k, shape: 1.0 + 0.02 * jax.random.normal(k, shape, f32)
    dt0 = jnp.exp(jax.random.uniform(ks[8], (DEPTH, SSD_HEADS), f32, float(np.log(1e-3)), float(np.log(1e-1))))
    return {
        "x": jax.random.normal(ks[0], (BATCH, SEQ, D_MODEL), f32),
        "attn_norm": gain(ks[1], (DEPTH, D_MODEL)),
        "w_in": nrm(ks[2], (DEPTH, D_MODEL, IN_PROJ_DIM), D_MODEL ** -0.5),
        "sb_q_norm": gain(ks[3], (DEPTH, SB_HEAD_DIM)),
        "sb_k_norm": gain(ks[4], (DEPTH, SB_HEAD_DIM)),
        "conv_w": nrm(ks[5], (DEPTH, SSD_CONV, 1, SSD_CONV_DIM), SSD_CONV ** -0.5),
        "conv_b": nrm(ks[6], (DEPTH, SSD_CONV_DIM), 0.02),
        "dt_bias": jnp.log(jnp.expm1(dt0)),
        "a_log": jnp.log(jax.random.uniform(ks[9], (DEPTH, SSD_HEADS), f32, 1.0, 16.0)),
        "d_skip": gain(ks[10], (DEPTH, SSD_HEADS)),
        "sb_out_norm": gain(ks[11], (DEPTH, SB_WIDTH)),
        "ssd_out_norm": gain(ks[12], (DEPTH, SSD_WIDTH)),
        "w_out": nrm(ks[13], (DEPTH, MIX_WIDTH, D_MODEL), MIX_WIDTH ** -0.5),
        "ffn_norm": gain(ks[14], (DEPTH, D_MODEL)),
        "peer_query": nrm(ks[15], (DEPTH, D_MODEL, PEER_HEADS * PEER_KEY_DIM), D_MODEL ** -0.5),
        "peer_sub_keys": nrm(ks[16], (DEPTH, PEER_HEADS, 2, PEER_N_KEYS, PEER_HALF), PEER_HALF ** -0.5),
        "peer_down": nrm(ks[17], (DEPTH, PEER_N_EXPERTS, D_MODEL), D_MODEL ** -0.5),
        "peer_up": nrm(ks[18], (DEPTH, PEER_N_EXPERTS, D_MODEL), (PEER_HEADS * PEER_TOPK) ** -0.5),
    }


def reference(x, attn_norm, w_in, sb_q_norm, sb_k_norm, conv_w, conv_b, dt_bias, a_log, d_skip,
              sb_out_norm, ssd_out_norm, w_out, ffn_norm, peer_query, peer_sub_keys, peer_down, peer_up):
    h = x
    for layer in range(DEPTH):
        xn = rms_norm(h, attn_norm[layer])
        h = h + hybrid_mixer(xn, w_in[layer], sb_q_norm[layer], sb_k_norm[layer], conv_w[layer],
                             conv_b[layer], dt_bias[layer], a_log[layer], d_skip[layer],
                             sb_out_norm[layer], ssd_out_norm[layer], w_out[layer])
        hn = rms_norm(h, ffn_norm[layer])
        h = h + peer_ffn(hn, peer_query[layer], peer_sub_keys[layer], peer_down[layer], peer_up[layer])
    return h
```

```python
import os
import numpy as np
from contextlib import ExitStack
import concourse.bass as bass
import concourse.mybir as mybir
from concourse.bass_utils import run_bass_kernel_spmd

F32 = mybir.dt.float32
BF16 = mybir.dt.bfloat16
AF = mybir.ActivationFunctionType
ALU = mybir.AluOpType
AX = mybir.AxisListType

D = 2048
NE = 16384
EPS = 1e-6
NEG = -1.0e30


class Buf:
    __slots__ = ("w", "r", "x")

    def __init__(self, x=False):
        self.w = None
        self.r = []
        self.x = x


def PB():
    return Buf(True)


class Sched:
    NDMA = 8

    def __init__(self, nc, stack):
        self.nc = nc
        self.eng = {"pe": nc.tensor, "dve": nc.vector, "act": nc.scalar,
                    "pool": nc.gpsimd, "sp": nc.sync}
        self.semh = {}
        self.cnt = {}
        self.seen = {k: {} for k in self.eng}
        for k in self.eng:
            self.semh[k] = stack.enter_context(nc.semaphore("s_" + k))
            self.cnt[k] = 0
        self.dma_rr = {}
        for q in ("sp", "act", "pool"):
            self.dma_rr[q] = 0
            for i in range(self.NDMA):
                key = "d_%s_%d" % (q, i)
                self.semh[key] = stack.enter_context(nc.semaphore(key))
                self.cnt[key] = 0
        self.semh["cc"] = stack.enter_context(nc.semaphore("s_cc"))
        self.cnt["cc"] = 0
        self.prog = {k: [] for k in self.eng}
        self.n_ins = 0

    def _waits(self, e, deps):
        need = {}
        for (k, v) in deps:
            if v > need.get(k, 0):
                need[k] = v
        seen = self.seen[e]
        for k, v in need.items():
            if k == e and e == "pe":
                continue
            if seen.get(k, 0) >= v:
                continue
            self.prog[e].append((0, k, v))
            seen[k] = v

    @staticmethod
    def _deps(reads, writes):
        deps = []
        for b in reads:
            if b.w is not None:
                deps.append(b.w)
        for b in writes:
            deps.extend(b.r)
            if b.w is not None:
                deps.append(b.w)
        return deps

    @staticmethod
    def _record(ev, reads, writes):
        for b in reads:
            b.r.append(ev)
            if len(b.r) > 48:
                m = {}
                for (k, v) in b.r:
                    if v > m.get(k, 0):
                        m[k] = v
                b.r = list(m.items())
        for b in writes:
            b.w = ev
            b.r = []

    def op(self, e, fn, reads=(), writes=()):
        if any(b.x for b in reads):
            writes = list(writes) + [b for b in reads if b.x]
            reads = [b for b in reads if not b.x]
        self._waits(e, self._deps(reads, writes))
        self.cnt[e] += 1
        self.prog[e].append((1, fn, e, 1))
        self.n_ins += 1
        ev = (e, self.cnt[e])
        self._record(ev, reads, writes)
        return ev

    def dma(self, q, out, in_, reads=(), writes=()):
        i = self.dma_rr[q]
        self.dma_rr[q] = (i + 1) % self.NDMA
        key = "d_%s_%d" % (q, i)
        deps = self._deps(reads, writes)
        if self.cnt[key] > 0:
            deps.append((key, self.cnt[key]))
        self._waits(q, deps)
        self.cnt[key] += 16
        self.prog[q].append((1, (lambda eng, o=out, i_=in_: eng.dma_start(out=o, in_=i_)), key, 16))
        self.n_ins += 1
        ev = (key, self.cnt[key])
        self._record(ev, reads, writes)
        return ev

    def collective(self, fn, reads=(), writes=()):
        self._waits("pool", self._deps(reads, writes))
        self.cnt["cc"] += 1
        self.prog["pool"].append((1, fn, "cc", 1))
        ev = ("cc", self.cnt["cc"])
        self._record(ev, reads, writes)
        return ev

    def wait_all(self, e, bufs):
        deps = []
        for b in bufs:
            deps.extend(b.r)
            if b.w is not None:
                deps.append(b.w)
        self._waits(e, deps)

    def barrier(self):
        for e in self.eng:
            self._waits(e, [(k, v) for k, v in self.cnt.items() if v > 0 and not (k == e and e == "pe")])

    def emit(self):
        self.barrier()
        nc = self.nc
        semh = self.semh
        prog = self.prog
        self.prog = {k: [] for k in self.eng}

        def run(eng, lst):
            for it in lst:
                if it[0] == 0:
                    eng.wait_ge(semh[it[1]], it[2])
                else:
                    it[1](eng).then_inc(semh[it[2]], it[3])

        with nc.Block() as block:
            if prog["sp"]:
                block.sync(lambda eng: run(eng, prog["sp"]))
            if prog["pe"]:
                block.tensor(lambda eng: run(eng, prog["pe"]))
            if prog["dve"]:
                block.vector(lambda eng: run(eng, prog["dve"]))
            if prog["act"]:
                block.scalar(lambda eng: run(eng, prog["act"]))
            if prog["pool"]:
                block.gpsimd(lambda eng: run(eng, prog["pool"]))


class Ctx:
    pass


def build_program(SEQ, debug=False):
    NT = SEQ // 128
    NG = SEQ // 512
    T4 = SEQ // 4
    NT4 = T4 // 128
    nc = bass.Bass("TRN2", target_bir_lowering=False)
    dram = lambda n, s, d, k: nc.dram_tensor(n, s, d, kind=k).ap()
    g = Ctx()
    g.nc = nc
    g.SEQ, g.NT, g.NG, g.T4, g.NT4 = SEQ, NT, NG, T4, NT4
    g.x = dram("x", [SEQ, D], F32, "ExternalInput")
    g.xq = dram("xq", [T4, D], F32, "ExternalInput")
    g.w1 = dram("w1", [D, 1540], F32, "ExternalInput")
    g.gn1 = dram("gn1", [128, 16], F32, "ExternalInput")
    g.gqk = dram("gqk", [128, 2], F32, "ExternalInput")
    g.cw = dram("cw", [128, 16], F32, "ExternalInput")
    g.cb = dram("cb", [128, 4], F32, "ExternalInput")
    g.ssdp = dram("ssdp", [1, 12], F32, "ExternalInput")
    g.dsk = dram("dsk", [1, 256], F32, "ExternalInput")
    g.gso = dram("gso", [128, 16], F32, "ExternalInput")
    g.wo = dram("wo", [D, D], F32, "ExternalInput")
    g.gn2 = dram("gn2", [128, 16], F32, "ExternalInput")
    g.wq = dram("wq", [D, D], F32, "ExternalInput")
    g.keysT = dram("keysT", [128, 16 * 128], F32, "ExternalInput")
    big = "t" in os.environ.get("KPH", "tpacx")
    g.down = dram("down", [NE if big else 128, D], F32, "ExternalInput")
    g.up = dram("up", [NE if big else 128, D], F32, "ExternalInput")
    g.y = dram("y", [T4, D], F32, "ExternalOutput")
    g.qTs = dram("qTs", [2, 128, SEQ], BF16, "Internal")
    g.kTs = dram("kTs", [2, 128, SEQ], BF16, "Internal")
    g.vs = dram("vs", [SEQ, 256], BF16, "Internal")
    g.xch = dram("xch", [SEQ, 512], F32, "Internal")
    g.xin = dram("xin", [4 * SEQ, 512], F32, "Internal")
    g.sel = dram("sel", [1, 4], F32, "ExternalInput")
    if big:
        g.dTs = g.down.bitcast(BF16).rearrange("(e r) c -> e (r c)", r=512)[:, 0:128 * 8192].rearrange("e (p k) -> e p k", p=128)
        g.ups = g.up.bitcast(BF16).rearrange("(e r) c -> e (r c)", r=128)[:, 0:128 * D].rearrange("e (p k) -> e p k", p=128)
    else:
        g.dTs = dram("dTs", [NE // 512, 128, 16 * 512], BF16, "Internal")
        g.ups = dram("ups", [NE // 128, 128, D], BF16, "Internal")
    if debug:
        g.dbg_h = dram("dbg_h", [T4, D], F32, "ExternalOutput")
        g.dbg_x = dram("dbg_x", [SEQ, 512], F32, "ExternalOutput")
    g.b_qTs, g.b_kTs, g.b_vs, g.b_xch = Buf(), Buf(), Buf(), Buf()
    g.b_dTs, g.b_ups, g.b_y = Buf(), Buf(), Buf()
    g.b_dbg = Buf()

    ph = os.environ.get("KPH", "tpacx")
    g.ph = ph
    with ExitStack() as st0:
        S = Sched(nc, st0)
        g.S = S
        consts(g, st0)
        S.emit()
        g.b_dTs = [Buf() for _ in range(NE // 512)]
        g.b_ups = [Buf() for _ in range(NE // 128)]
        g.b_q = [Buf() for _ in range(NG)]
        g.b_k = [Buf() for _ in range(NG)]
        g.b_v = [Buf() for _ in range(NT)]
        g.b_xs = [Buf() for _ in range(NT)]
        g.b_xa = [Buf() for _ in range(NT)]
        g.b_xin = [Buf() for _ in range(SEQ // 256)]
        if "t" in ph:
            with ExitStack() as st:
                phase_tables(g, st)
                S.emit()
        if "p" in ph:
            with ExitStack() as st:
                phase_proj_ssd(g, st)
                S.emit()
        if "a" in ph:
            with ExitStack() as st:
                phase_attn(g, st)
                S.emit()
        if debug and "a" in ph:
            S.dma("sp", g.dbg_x[:, 0:256], g.xch[:, 0:256], reads=g.b_xa, writes=[g.b_dbg])
        if debug and "p" in ph:
            S.dma("sp", g.dbg_x[:, 256:512], g.xch[:, 256:512], reads=g.b_xs, writes=[g.b_dbg])
        S.emit()
        with ExitStack() as st:
            if "x" in ph:
                phase_tokens(g, st)
            S.wait_all("sp", [g.b_y, g.b_dbg] + g.b_xin + g.b_dTs + g.b_ups)
            S.emit()
    return nc


def consts(g, st):
    nc, S = g.nc, g.S
    sb = lambda n, s, d: st.enter_context(nc.sbuf_tensor(n, s, d))
    g.ones_f = sb("ones_f", [128, 128], F32)
    g.ones_b = sb("ones_b", [128, 128], BF16)
    g.ident_f = sb("ident_f", [128, 128], F32)
    g.ident_b = sb("ident_b", [128, 128], BF16)
    g.tri_f = sb("tri_f", [128, 128], F32)
    g.trl_b = sb("trl_b", [128, 128], BF16)
    g.cm_f = sb("cm_f", [128, 4, 512], F32)
    tmp = sb("c_tmp", [128, 128], F32)
    b = g.bc = Buf()
    bt = Buf()
    S.op("pool", lambda e: e.memset(g.ones_f[:], 1.0), writes=[b])
    S.op("pool", lambda e: e.memset(g.ones_b[:], 1.0), writes=[b])
    S.op("pool", lambda e: e.affine_select(out=g.ident_f[:], in_=g.ones_f[:], pattern=[[-1, 128]],
                                           compare_op=ALU.is_equal, fill=0.0, base=0, channel_multiplier=1),
         reads=[b], writes=[b])
    S.op("pool", lambda e: e.tensor_copy(out=g.ident_b[:], in_=g.ident_f[:]), reads=[b], writes=[b])
    S.op("pool", lambda e: e.affine_select(out=g.tri_f[:], in_=g.ones_f[:], pattern=[[1, 128]],
                                           compare_op=ALU.is_ge, fill=0.0, base=0, channel_multiplier=-1),
         reads=[b], writes=[b])
    S.op("pool", lambda e: e.affine_select(out=tmp[:], in_=g.ones_f[:], pattern=[[-1, 128]],
                                           compare_op=ALU.is_ge, fill=0.0, base=-1, channel_multiplier=1),
         reads=[b], writes=[bt])
    S.op("pool", lambda e: e.tensor_copy(out=g.trl_b[:], in_=tmp[:]), reads=[bt], writes=[b])
    for d in range(4):
        S.op("pool", lambda e, d=d: e.memset(g.cm_f[:, d, :], 1.0), writes=[b])
        S.op("pool", lambda e, d=d: e.affine_select(out=g.cm_f[:, d, :], in_=g.cm_f[:, d, :], pattern=[[1, 512]],
                                                    compare_op=ALU.is_ge, fill=0.0, base=-(d * 128) - 1,
                                                    channel_multiplier=-1), reads=[b], writes=[b])


def rstd_from_ssq(S, ssq, rstd, n, bs, br):
    S.op("act", lambda e: e.activation(out=rstd, in_=ssq, func=AF.Sqrt, bias=EPS, scale=1.0 / n),
         reads=[bs], writes=[br])
    S.op("dve", lambda e: e.reciprocal(out=rstd, in_=rstd), reads=[br], writes=[br])


def phase_tables(g, st):
    nc, S = g.nc, g.S
    sb = lambda n, s, d: st.enter_context(nc.sbuf_tensor(n, s, d))
    ps = lambda n, s, d: st.enter_context(nc.psum_tensor(n, s, d))
    ld = [sb("t_ld%d" % i, [128, D], F32) for i in range(2)]
    ldu = [sb("t_ldu%d" % i, [128, D], F32) for i in range(2)]
    ucv = [sb("t_ucv%d" % i, [128, D], BF16) for i in range(2)]
    stg = [sb("t_stg%d" % i, [128, 16, 512], BF16) for i in range(2)]
    pT = [ps("t_pT%d" % i, [128, 4, 128], F32) for i in range(4)]
    bld, bldu, bucv, bstg = [Buf(), Buf()], [Buf(), Buf()], [Buf(), Buf()], [Buf(), Buf()]
    bpT = [PB() for _ in range(4)]
    for eb in range(NE // 128):
        i = eb % 2
        s = (eb // 4) % 2
        S.dma("sp", ld[i][:], g.down[eb * 128:(eb + 1) * 128, :], writes=[bld[i]])
        for r in range(4):
            for c in range(4):
                kc = r * 4 + c
                S.op("pe", lambda e, r=r, c=c, kc=kc, i=i: e.transpose(
                    out=pT[r][:, c, :], in_=ld[i][:, kc * 128:(kc + 1) * 128], identity=g.ident_f[:]),
                    reads=[bld[i], g.bc], writes=[bpT[r]])
            dst = stg[s][:, r * 4:(r + 1) * 4, (eb % 4) * 128:(eb % 4 + 1) * 128]
            if r % 2 == 0:
                S.op("act", lambda e, r=r, dst=dst: e.activation(out=dst, in_=pT[r][:], func=AF.Copy),
                     reads=[bpT[r]], writes=[bstg[s]])
            else:
                S.op("dve", lambda e, r=r, dst=dst: e.tensor_copy(out=dst, in_=pT[r][:]),
                     reads=[bpT[r]], writes=[bstg[s]])
        if eb % 4 == 3:
            S.dma("pool", g.dTs[eb // 4].rearrange("p (c e) -> p c e", c=16), stg[s][:],
                  reads=[bstg[s]], writes=[g.b_dTs[eb // 4]])
        S.dma("sp", ldu[i][:], g.up[eb * 128:(eb + 1) * 128, :], writes=[bldu[i]])
        S.op("pool", lambda e, i=i: e.tensor_copy(out=ucv[i][:], in_=ldu[i][:]), reads=[bldu[i]], writes=[bucv[i]])
        S.dma("pool", g.ups[eb], ucv[i][:], reads=[bucv[i]], writes=[g.b_ups[eb]])


def phase_proj_ssd(g, st):
    nc, S, NG = g.nc, g.S, g.NG
    sb = lambda n, s, d: st.enter_context(nc.sbuf_tensor(n, s, d))
    ps = lambda n, s, d: st.enter_context(nc.psum_tensor(n, s, d))
    B = Buf
    Wb = sb("p1_Wb", [128, 16, 1540], BF16)
    bWb = B()
    wst = [sb("p1_wst%d" % i, [128, 1540], F32) for i in range(2)]
    bwst = [B(), B()]
    for kc in range(16):
        i = kc % 2
        S.dma("sp", wst[i][:], g.w1[kc * 128:(kc + 1) * 128, :], writes=[bwst[i]])
        S.op("dve", lambda e, kc=kc, i=i: e.tensor_copy(out=Wb[:, kc, :], in_=wst[i][:]), reads=[bwst[i]], writes=[bWb])
    gn1 = sb("p1_gn1", [128, 16], F32)
    gqk = sb("p1_gqk", [128, 2], F32)
    cw = sb("p1_cw", [128, 4, 4], F32)
    cb = sb("p1_cb", [128, 4], F32)
    ssdp = sb("p1_ssdp", [128, 12], F32)
    aneg = sb("p1_aneg", [128, 4], F32)
    dsk = sb("p1_dsk", [128, 256], F32)
    bp = B()
    S.dma("sp", gn1[:], g.gn1, writes=[bp])
    S.dma("sp", gqk[:], g.gqk, writes=[bp])
    S.dma("sp", cw[:], g.cw.rearrange("p (c j) -> p c j", c=4), writes=[bp])
    S.dma("sp", cb[:], g.cb, writes=[bp])
    S.dma("sp", ssdp[:], g.ssdp.partition_broadcast(128), writes=[bp])
    S.dma("sp", dsk[:], g.dsk.partition_broadcast(128), writes=[bp])
    S.op("dve", lambda e: e.tensor_scalar(out=gqk[:, 0:1], in0=gqk[:, 0:1], scalar1=float(128.0 ** -0.5), scalar2=None,
                                          op0=ALU.mult), reads=[bp], writes=[bp])
    S.op("act", lambda e: e.activation(out=aneg[:], in_=ssdp[:, 4:8], func=AF.Exp), reads=[bp], writes=[bp])
    S.op("dve", lambda e: e.tensor_scalar(out=aneg[:], in0=aneg[:], scalar1=-1.0, scalar2=None, op0=ALU.mult),
         reads=[bp], writes=[bp])
    dtb = ssdp[:, 0:4]
    prevT = sb("p1_prevT", [128, 4, 64], F32)
    prev_b = sb("p1_prevb", [128, 4, 64], BF16)
    bprev, bprevb = B(), B()
    S.op("pool", lambda e: e.memset(prevT[:], 0.0), writes=[bprev])
    S.op("pool", lambda e: e.memset(prev_b[:], 0.0), writes=[bprevb])
    cin = [sb("p1_cin%d" % i, [128, 515], F32) for i in range(4)]
    bcin = [B() for _ in range(4)]
    for i in range(4):
        S.op("pool", lambda e, i=i: e.memset(cin[i][:, 0:3], 0.0), writes=[bcin[i]])
    xld = [sb("p1_xld%d" % i, [128, D], F32) for i in range(2)]
    bxld = [B(), B()]
    junk = sb("p1_junk", [128, D], BF16)
    bjunk = B()
    xs = sb("p1_xs", [128, D], F32)
    bxs = B()
    ssq = sb("p1_ssq", [128, 2], F32)
    bssq = B()
    xnT = [sb("p1_xnT%d" % i, [128, 16, 512], BF16) for i in range(2)]
    bxnT = [B(), B()]
    sqf = sb("p1_sqf", [128, 512], F32)
    bsqf = B()
    rsb = sb("p1_rsb", [128, 512], F32)
    brsb = B()
    qko = [sb("p1_qko%d" % i, [128, 512], BF16) for i in range(2)]
    bqko = [B(), B()]
    cacc = sb("p1_cacc", [128, 512], F32)
    bcacc = B()
    xcf = [sb("p1_xcf%d" % i, [128, 512], F32) for i in range(2)]
    Bcf = sb("p1_Bcf", [128, 512], F32)
    Bc = sb("p1_Bc", [128, 512], BF16)
    Cc = sb("p1_Cc", [128, 512], BF16)
    bxc = B()
    vb = [sb("p1_vb%d" % i, [128, 256], BF16) for i in range(2)]
    bvb = [B(), B()]
    zs = sb("p1_zs", [128, 256], F32)
    bzs = B()
    dt4 = sb("p1_dt4", [128, 4], F32)
    adt = sb("p1_adt", [128, 4], F32)
    acol = sb("p1_acol", [128, 4], F32)
    w4 = sb("p1_w4", [128, 4], F32)
    bsm = B()
    adt_rep = sb("p1_adtrep", [128, 4, 128], F32)
    badr = B()
    xtm = sb("p1_xtm", [128, 256], F32)
    Btm = sb("p1_Btm", [128, 128], BF16)
    xdt = sb("p1_xdt", [128, 4, 64], BF16)
    xw = sb("p1_xw", [128, 4, 64], BF16)
    bxtm, bBtm, bxdt, bxw = B(), B(), B(), B()
    cbm = sb("p1_cbm", [128, 128], F32)
    bcbm = B()
    arg = sb("p1_arg", [128, 4, 128], F32)
    barg = B()
    Mb = sb("p1_Mb", [128, 4, 128], BF16)
    bMb = B()
    eacs = sb("p1_eacs", [128, 4, 128], F32)
    beacs = B()
    Cdec = sb("p1_Cdec", [128, 4, 128], BF16)
    bCdec = B()
    yo = [sb("p1_yo%d" % i, [128, 256], F32) for i in range(2)]
    byo = [B(), B()]
    yt = sb("p1_yt", [128, 256], F32)
    byt = B()
    pT = [ps("p1_pT%d" % i, [128, 4, 128], F32) for i in range(2)]
    bpT = [PB(), PB()]
    pf = ps("p1_pf", [128, 512], F32)
    bpf = PB()
    psm = ps("p1_psm", [128, 512], F32)
    bpsm = PB()
    pv = ps("p1_pv", [128, 512], F32)
    bpv = PB()
    pm = ps("p1_pm", [128, 512], F32)
    bpm = PB()
    pacs = ps("p1_pacs", [128, 4, 128], F32)
    bpacs = PB()
    py = ps("p1_py", [128, 512], F32)
    bpy = PB()

    for gi in range(NG):
        xT = xnT[gi % 2]
        bxT = bxnT[gi % 2]
        for j in range(4):
            t = gi * 4 + j
            xt, bx = xld[t % 2], bxld[t % 2]
            S.dma("sp", xt[:], g.x[t * 128:(t + 1) * 128, :], writes=[bx])
            S.op("act", lambda e, xt=xt: e.activation(out=junk[:], in_=xt[:], func=AF.Square, accum_out=ssq[:, 0:1]),
                 reads=[bx], writes=[bjunk, bssq])
            rstd_from_ssq(S, ssq[:, 0:1], ssq[:, 1:2], D, bssq, bssq)
            S.op("act", lambda e, xt=xt: e.activation(out=xs[:], in_=xt[:], func=AF.Copy, scale=ssq[:, 1:2]),
                 reads=[bx, bssq], writes=[bxs])
            for r in range(4):
                p_, bp_ = pT[r % 2], bpT[r % 2]
                for c in range(4):
                    kc = r * 4 + c
                    S.op("pe", lambda e, p_=p_, c=c, kc=kc: e.transpose(
                        out=p_[:, c, :], in_=xs[:, kc * 128:(kc + 1) * 128], identity=g.ident_f[:]),
                        reads=[bxs, g.bc], writes=[bp_])
                S.op("dve", lambda e, p_=p_, r=r, j=j, xT=xT: e.tensor_tensor(
                    out=xT[:, r * 4:(r + 1) * 4, j * 128:(j + 1) * 128], in0=p_[:],
                    in1=gn1[:, r * 4:(r + 1) * 4].unsqueeze(2).to_broadcast([128, 4, 128]), op=ALU.mult),
                    reads=[bp_, bp], writes=[bxT])
        for oc in range(8):
            for kc in range(16):
                S.op("pe", lambda e, kc=kc, oc=oc, xT=xT: e.matmul(
                    pf[:], lhsT=Wb[:, kc, oc * 128:(oc + 1) * 128], rhs=xT[:, kc, :], start=(kc == 0), stop=(kc == 15)),
                    reads=[bWb, bxT], writes=[bpf])
            if oc < 4:
                isq, hd = oc < 2, oc % 2
                S.op("act", lambda e: e.activation(out=sqf[:], in_=pf[:], func=AF.Square), reads=[bpf], writes=[bsqf])
                S.op("pe", lambda e: e.matmul(psm[:], lhsT=g.ones_f[:], rhs=sqf[:], start=True, stop=True),
                     reads=[bsqf, g.bc], writes=[bpsm])
                S.op("act", lambda e: e.activation(out=rsb[:], in_=psm[:], func=AF.Sqrt, bias=EPS, scale=1.0 / 128),
                     reads=[bpsm], writes=[brsb])
                S.op("dve", lambda e: e.reciprocal(out=rsb[:], in_=rsb[:]), reads=[brsb], writes=[brsb])
                qo, bq = qko[oc % 2], bqko[oc % 2]
                gcol = gqk[:, 0:1] if isq else gqk[:, 1:2]
                S.op("dve", lambda e, qo=qo, gcol=gcol: e.scalar_tensor_tensor(
                    out=qo[:], in0=pf[:], scalar=gcol, in1=rsb[:], op0=ALU.mult, op1=ALU.mult),
                    reads=[bpf, brsb, bp], writes=[bq])
                dst = (g.qTs if isq else g.kTs)[hd][:, gi * 512:(gi + 1) * 512]
                S.dma("pool", dst, qo[:], reads=[bq], writes=[(g.b_q if isq else g.b_k)[gi]] if hd == 1 else
                      [B()] if False else [(g.b_q if isq else g.b_k)[gi]])
            else:
                ch = oc - 4
                ci, bci = cin[ch], bcin[ch]
                S.op("act", lambda e, ci=ci: e.activation(out=ci[:, 3:515], in_=pf[:], func=AF.Copy),
                     reads=[bpf], writes=[bci])
                S.op("dve", lambda e, ci=ci, ch=ch: e.tensor_scalar(
                    out=cacc[:], in0=ci[:, 0:512], scalar1=cw[:, ch, 0:1], scalar2=cb[:, ch:ch + 1],
                    op0=ALU.mult, op1=ALU.add), reads=[bci, bp], writes=[bcacc])
                for jt in range(1, 4):
                    S.op("dve", lambda e, ci=ci, ch=ch, jt=jt: e.scalar_tensor_tensor(
                        out=cacc[:], in0=ci[:, jt:jt + 512], scalar=cw[:, ch, jt:jt + 1], in1=cacc[:],
                        op0=ALU.mult, op1=ALU.add), reads=[bci, bp, bcacc], writes=[bcacc])
                if ch < 2:
                    S.op("act", lambda e, ch=ch: e.activation(out=xcf[ch][:], in_=cacc[:], func=AF.Silu),
                         reads=[bcacc], writes=[bxc])
                elif ch == 2:
                    S.op("act", lambda e: e.activation(out=Bcf[:], in_=cacc[:], func=AF.Silu), reads=[bcacc], writes=[bxc])
                    S.op("dve", lambda e: e.tensor_copy(out=Bc[:], in_=Bcf[:]), reads=[bxc], writes=[bxc])
                else:
                    S.op("act", lambda e: e.activation(out=Cc[:], in_=cacc[:], func=AF.Silu), reads=[bcacc], writes=[bxc])
                S.op("dve", lambda e, ci=ci: e.tensor_copy(out=ci[:, 0:3], in_=ci[:, 512:515]), reads=[bci], writes=[bci])
        for j in range(4):
            t = gi * 4 + j
            cs = slice(j * 128, (j + 1) * 128)
            for kc in range(16):
                S.op("pe", lambda e, kc=kc, cs=cs, xT=xT: e.matmul(
                    pv[:], lhsT=xT[:, kc, cs], rhs=Wb[:, kc, 1024:1536], start=(kc == 0), stop=(kc == 15)),
                    reads=[bWb, bxT], writes=[bpv])
            for kc in range(16):
                S.op("pe", lambda e, kc=kc, cs=cs, xT=xT: e.matmul(
                    psm[:, 0:4], lhsT=xT[:, kc, cs], rhs=Wb[:, kc, 1536:1540], start=(kc == 0), stop=(kc == 15)),
                    reads=[bWb, bxT], writes=[bpsm])
            vbt, bvt = vb[t % 2], bvb[t % 2]
            S.op("act", lambda e, vbt=vbt: e.activation(out=vbt[:], in_=pv[:, 0:256], func=AF.Copy), reads=[bpv], writes=[bvt])
            S.dma("pool", g.vs[t * 128:(t + 1) * 128, :], vbt[:], reads=[bvt], writes=[g.b_v[t]])
            S.op("act", lambda e: e.activation(out=zs[:], in_=pv[:, 256:512], func=AF.Silu), reads=[bpv], writes=[bzs])
            S.op("dve", lambda e: e.tensor_tensor(out=dt4[:], in0=psm[:, 0:4], in1=dtb, op=ALU.add),
                 reads=[bpsm, bp], writes=[bsm])
            S.op("act", lambda e: e.activation(out=dt4[:], in_=dt4[:], func=AF.Exp), reads=[bsm], writes=[bsm])
            S.op("act", lambda e: e.activation(out=dt4[:], in_=dt4[:], func=AF.Ln, bias=1.0), reads=[bsm], writes=[bsm])
            S.op("dve", lambda e: e.tensor_tensor(out=adt[:], in0=dt4[:], in1=aneg[:], op=ALU.mult),
                 reads=[bsm, bp], writes=[bsm])
            S.op("pe", lambda e: e.matmul(psm[:, 4:8], lhsT=g.tri_f[:], rhs=adt[:], start=True, stop=True),
                 reads=[bsm, g.bc], writes=[bpsm])
            S.op("act", lambda e: e.activation(out=acol[:], in_=psm[:, 4:8], func=AF.Copy), reads=[bpsm], writes=[bsm])
            S.op("dve", lambda e: e.tensor_copy(out=adt_rep[:], in_=adt[:].unsqueeze(2).to_broadcast([128, 4, 128])),
                 reads=[bsm], writes=[badr])
            for h in range(4):
                S.op("pe", lambda e, h=h: e.matmul(pacs[:, h, :], lhsT=adt_rep[:, h, :], rhs=g.tri_f[:], start=True, stop=True),
                     reads=[badr, g.bc], writes=[bpacs])
            for cc in range(2):
                S.op("pe", lambda e, cc=cc, cs=cs: e.transpose(out=pm[:, cc * 128:(cc + 1) * 128], in_=xcf[cc][:, cs],
                                                             identity=g.ident_f[:]), reads=[bxc, g.bc], writes=[bpm])
            S.op("pe", lambda e, cs=cs: e.transpose(out=pm[:, 256:384], in_=Bcf[:, cs], identity=g.ident_f[:]),
                 reads=[bxc, g.bc], writes=[bpm])
            S.op("pe", lambda e, cs=cs: e.matmul(pm[:, 384:512], lhsT=Bc[:, cs], rhs=Cc[:, cs], start=True, stop=True),
                 reads=[bxc], writes=[bpm])
            S.op("act", lambda e: e.activation(out=xtm[:], in_=pm[:, 0:256], func=AF.Copy), reads=[bpm], writes=[bxtm])
            S.op("act", lambda e: e.activation(out=Btm[:], in_=pm[:, 256:384], func=AF.Copy), reads=[bpm], writes=[bBtm])
            S.op("dve", lambda e: e.tensor_tensor(out=cbm[:], in0=pm[:, 384:512], in1=g.tri_f[:], op=ALU.mult),
                 reads=[bpm, g.bc], writes=[bcbm])
            xtm3 = xtm[:].rearrange("p (h d) -> p h d", h=4)
            S.op("dve", lambda e, xtm3=xtm3: e.tensor_tensor(
                out=xdt[:], in0=xtm3, in1=dt4[:].unsqueeze(2).to_broadcast([128, 4, 64]), op=ALU.mult),
                reads=[bxtm, bsm], writes=[bxdt])
            S.op("dve", lambda e: e.tensor_tensor(out=arg[:], in0=pacs[:], in1=acol[:].unsqueeze(2).to_broadcast([128, 4, 128]),
                                                  op=ALU.subtract), reads=[bpacs, bsm], writes=[barg])
            S.op("dve", lambda e: e.tensor_scalar(out=arg[:], in0=arg[:], scalar1=0.0, scalar2=None, op0=ALU.min),
                 reads=[barg], writes=[barg])
            S.op("act", lambda e: e.activation(out=arg[:], in_=arg[:], func=AF.Exp), reads=[barg], writes=[barg])
            S.op("dve", lambda e: e.tensor_tensor(out=Mb[:], in0=arg[:], in1=cbm[:].unsqueeze(1).to_broadcast([128, 4, 128]),
                                                  op=ALU.mult), reads=[barg, bcbm], writes=[bMb])
            S.op("act", lambda e: e.activation(out=eacs[:], in_=pacs[:], func=AF.Exp), reads=[bpacs], writes=[beacs])
            S.op("dve", lambda e, cs=cs: e.tensor_tensor(out=Cdec[:], in0=eacs[:],
                                                         in1=Cc[:, cs].unsqueeze(1).to_broadcast([128, 4, 128]), op=ALU.mult),
                 reads=[beacs, bxc], writes=[bCdec])
            for h in range(4):
                S.op("pe", lambda e, h=h: e.matmul(py[:, h * 64:(h + 1) * 64], lhsT=Mb[:, h, :], rhs=xdt[:, h, :],
                                                   start=True, stop=False), reads=[bMb, bxdt], writes=[bpy])
                S.op("pe", lambda e, h=h: e.matmul(py[:, h * 64:(h + 1) * 64], lhsT=Cdec[:, h, :], rhs=prev_b[:, h, :],
                                                   start=False, stop=True), reads=[bCdec, bprevb], writes=[bpy])
            S.op("dve", lambda e: e.tensor_tensor(out=w4[:], in0=pacs[:, :, 127], in1=acol[:], op=ALU.subtract),
                 reads=[bpacs, bsm], writes=[bsm])
            S.op("act", lambda e: e.activation(out=w4[:], in_=w4[:], func=AF.Exp), reads=[bsm], writes=[bsm])
            S.op("dve", lambda e: e.tensor_tensor(out=w4[:], in0=w4[:], in1=dt4[:], op=ALU.mult), reads=[bsm], writes=[bsm])
            S.op("dve", lambda e, xtm3=xtm3: e.tensor_tensor(
                out=xw[:], in0=xtm3, in1=w4[:].unsqueeze(2).to_broadcast([128, 4, 64]), op=ALU.mult),
                reads=[bxtm, bsm], writes=[bxw])
            S.op("pe", lambda e: e.matmul(py[:, 256:512], lhsT=Btm[:], rhs=xw[:].rearrange("p h d -> p (h d)"),
                                          start=True, stop=True), reads=[bBtm, bxw], writes=[bpy])
            S.op("dve", lambda e: e.tensor_tensor(out=yt[:], in0=xtm[:], in1=dsk[:], op=ALU.mult),
                 reads=[bxtm, bp], writes=[byt])
            S.op("dve", lambda e: e.tensor_tensor(out=yt[:], in0=py[:, 0:256], in1=yt[:], op=ALU.add),
                 reads=[bpy, byt], writes=[byt])
            yot, byot = yo[t % 2], byo[t % 2]
            S.op("dve", lambda e, yot=yot: e.tensor_tensor(out=yot[:], in0=yt[:], in1=zs[:], op=ALU.mult),
                 reads=[byt, bzs], writes=[byot])
            S.dma("pool", g.xch[t * 128:(t + 1) * 128, 256:512], yot[:], reads=[byot], writes=[g.b_xs[t]])
            for h in range(4):
                S.op("dve", lambda e, h=h: e.scalar_tensor_tensor(
                    out=prevT[:, h, :], in0=prevT[:, h, :], scalar=eacs[:, h, 127:128],
                    in1=py[:, 256 + h * 64:256 + (h + 1) * 64], op0=ALU.mult, op1=ALU.add),
                    reads=[bprev, beacs, bpy], writes=[bprev])
            S.op("act", lambda e: e.activation(out=prev_b[:], in_=prevT[:], func=AF.Copy), reads=[bprev], writes=[bprevb])


def phase_attn(g, st):
    nc, S, SEQ, NT, NG = g.nc, g.S, g.SEQ, g.NT, g.NG
    sb = lambda n, s, d: st.enter_context(nc.sbuf_tensor(n, s, d))
    ps = lambda n, s, d: st.enter_context(nc.psum_tensor(n, s, d))
    B = Buf
    kT = sb("a_kT", [128, SEQ], BF16)
    nkT = sb("a_nkT", [128, SEQ], BF16)
    vt = sb("a_vt", [128, NT, 128], BF16)
    bkT, bnkT, bvt = B(), B(), B()
    qg = [sb("a_q%d" % i, [128, 512], BF16) for i in range(2)]
    bqg = [B(), B()]
    ebuf = [sb("a_e%d" % i, [128, 512], F32) for i in range(2)]
    be = [B(), B()]
    spf = [sb("a_spf%d" % i, [128, 512], F32) for i in range(2)]
    bspf = [B(), B()]
    spb = [sb("a_spb%d" % i, [128, 512], BF16) for i in range(2)]
    bspb = [B(), B()]
    Srun = sb("a_S", [128, 512], BF16)
    bS = B()
    tb = [sb("a_t%d" % i, [128, 512], F32) for i in range(2)]
    btb = [B(), B()]
    wb = [sb("a_w%d" % i, [128, 512], BF16) for i in range(2)]
    bwb = [B(), B()]
    ao = [sb("a_ao%d" % i, [128, 4, 128], F32) for i in range(2)]
    bao = [B(), B()]
    pz = [ps("a_pz%d" % i, [128, 512], F32) for i in range(2)]
    pA = [ps("a_pA%d" % i, [128, 512], F32) for i in range(2)]
    po = [ps("a_po%d" % i, [128, 4, 128], F32) for i in range(2)]
    bpz, bpA, bpo = [PB(), PB()], [PB(), PB()], [PB(), PB()]
    cnt = 0
    for h2 in range(2):
        S.dma("sp", kT[:], g.kTs[h2], reads=g.b_k, writes=[bkT])
        S.op("dve", lambda e: e.tensor_scalar(out=nkT[:], in0=kT[:], scalar1=-1.0, scalar2=None, op0=ALU.mult),
             reads=[bkT], writes=[bnkT])
        for c0 in range(0, NT, 8):
            S.dma("sp", vt[:, c0:c0 + 8, :],
                  g.vs[c0 * 128:(c0 + 8) * 128, h2 * 128:(h2 + 1) * 128].rearrange("(n p) d -> p n d", p=128),
                  reads=g.b_v[c0:c0 + 8], writes=[bvt])
        for qgi in range(NG):
            qt, bq = qg[qgi % 2], bqg[qgi % 2]
            S.dma("sp", qt[:], g.qTs[h2][:, qgi * 512:(qgi + 1) * 512], reads=[g.b_q[qgi]], writes=[bq])
            pot, bpot = po[qgi % 2], bpo[qgi % 2]
            first = True
            for kb in range(4 * qgi + 3, -1, -1):
                d = kb - 4 * qgi
                i2 = cnt % 2
                cnt += 1
                ks = slice(kb * 128, (kb + 1) * 128)
                S.op("pe", lambda e, i2=i2, ks=ks, qt=qt: e.matmul(pz[i2][:], lhsT=kT[:, ks], rhs=qt[:], start=True, stop=True),
                     reads=[bkT, bq], writes=[bpz[i2]])
                S.op("pe", lambda e, i2=i2, ks=ks, qt=qt: e.matmul(pA[i2][:], lhsT=nkT[:, ks], rhs=qt[:], start=True, stop=False),
                     reads=[bnkT, bq], writes=[bpA[i2]])
                S.op("act", lambda e, i2=i2: e.activation(out=ebuf[i2][:], in_=pz[i2][:], func=AF.Exp),
                     reads=[bpz[i2]], writes=[be[i2]])
                S.op("act", lambda e, i2=i2: e.activation(out=spf[i2][:], in_=ebuf[i2][:], func=AF.Ln, bias=1.0),
                     reads=[be[i2]], writes=[bspf[i2]])
                if d >= 0:
                    S.op("pool", lambda e, i2=i2, d=d: e.tensor_tensor(out=spb[i2][:], in0=spf[i2][:], in1=g.cm_f[:, d, :],
                                                                      op=ALU.mult), reads=[bspf[i2], g.bc], writes=[bspb[i2]])
                else:
                    S.op("pool", lambda e, i2=i2: e.tensor_copy(out=spb[i2][:], in_=spf[i2][:]),
                         reads=[bspf[i2]], writes=[bspb[i2]])
                S.op("pe", lambda e, i2=i2, first=first: e.matmul(pA[i2][:], lhsT=g.trl_b[:], rhs=spb[i2][:], start=False,
                                                                   stop=first), reads=[bspb[i2], g.bc], writes=[bpA[i2]])
                if not first:
                    S.op("pe", lambda e, i2=i2: e.matmul(pA[i2][:], lhsT=g.ones_b[:], rhs=Srun[:], start=False, stop=True),
                         reads=[bS, g.bc], writes=[bpA[i2]])
                S.op("dve", lambda e, i2=i2: e.tensor_tensor(out=tb[i2][:], in0=pA[i2][:], in1=spf[i2][:], op=ALU.add),
                     reads=[bpA[i2], bspf[i2]], writes=[btb[i2]])
                S.op("act", lambda e, i2=i2: e.activation(out=wb[i2][:], in_=tb[i2][:], func=AF.Exp, scale=-1.0),
                     reads=[btb[i2]], writes=[bwb[i2]])
                if d >= 0:
                    S.op("pool", lambda e, i2=i2, d=d: e.tensor_tensor(out=wb[i2][:], in0=wb[i2][:], in1=g.cm_f[:, d, :],
                                                                      op=ALU.mult), reads=[bwb[i2], g.bc], writes=[bwb[i2]])
                for qb in range(4):
                    if d > qb:
                        continue
                    S.op("pe", lambda e, i2=i2, qb=qb, kb=kb, pot=pot, st_=(first and qb == 3): e.matmul(
                        pot[:, qb, :], lhsT=wb[i2][:, qb * 128:(qb + 1) * 128], rhs=vt[:, kb, :],
                        start=st_, stop=(kb == 0), skip_group_check=True),
                        reads=[bwb[i2], bvt], writes=[bpot])
                if first:
                    S.op("dve", lambda e, i2=i2: e.tensor_copy(out=Srun[:], in_=spb[i2][:]), reads=[bspb[i2]], writes=[bS])
                else:
                    S.op("dve", lambda e, i2=i2: e.tensor_tensor(out=Srun[:], in0=Srun[:], in1=spb[i2][:], op=ALU.add),
                         reads=[bspb[i2], bS], writes=[bS])
                first = False
            aot, baot = ao[qgi % 2], bao[qgi % 2]
            S.op("act", lambda e, aot=aot, pot=pot: e.activation(out=aot[:], in_=pot[:], func=AF.Copy),
                 reads=[bpot], writes=[baot])
            S.dma("pool", g.xch[qgi * 512:(qgi + 1) * 512, h2 * 128:(h2 + 1) * 128].rearrange("(b p) d -> p b d", p=128),
                  aot[:], reads=[baot], writes=g.b_xa[qgi * 4:(qgi + 1) * 4])
            if h2 == 1 and "c" in g.ph:
                for ck in (2 * qgi, 2 * qgi + 1):
                    S.collective(lambda eng, ck=ck: eng.collective_compute(
                        "AllGather", ALU.bypass, replica_groups=[[0, 1, 2, 3], [4, 5, 6, 7]],
                        ins=[g.xch[ck * 256:(ck + 1) * 256, :].opt()], outs=[g.xin[ck * 1024:(ck + 1) * 1024, :].opt()]),
                        reads=g.b_xs[2 * ck:2 * ck + 2] + g.b_xa[2 * ck:2 * ck + 2], writes=[g.b_xin[ck]])


def load_weight_bf16(g, sb, name, src, Wb, bWb):
    S = g.S
    wst = [sb(name + "_st%d" % i, [128, D], F32) for i in range(2)]
    bw = [Buf(), Buf()]
    for kc in range(16):
        i = kc % 2
        S.dma("sp", wst[i][:], src[kc * 128:(kc + 1) * 128, :], writes=[bw[i]])
        S.op("pool" if kc % 2 else "dve", lambda e, kc=kc, i=i: e.tensor_copy(out=Wb[:, kc, :], in_=wst[i][:]),
             reads=[bw[i]], writes=[bWb])


def norm_transpose(g, src, bsrc, gain, bgain, dstT, bdst, xs, bxs, pT, bpT):
    S = g.S
    for r in range(4):
        p_, bp_ = pT[r % 2], bpT[r % 2]
        for c in range(4):
            kc = r * 4 + c
            S.op("pe", lambda e, p_=p_, c=c, kc=kc: e.transpose(out=p_[:, c, :], in_=src[:, kc * 128:(kc + 1) * 128],
                                                              identity=g.ident_f[:]), reads=[bsrc, g.bc], writes=[bp_])
        S.op("dve", lambda e, p_=p_, r=r: e.tensor_tensor(
            out=dstT[:, r * 4:(r + 1) * 4, :], in0=p_[:],
            in1=gain[:, r * 4:(r + 1) * 4].unsqueeze(2).to_broadcast([128, 4, 128]), op=ALU.mult),
            reads=[bp_, bgain], writes=[bdst])


def phase_tokens(g, st0):
    nc, S, T4, NT4 = g.nc, g.S, g.T4, g.NT4
    NT4 = min(NT4, int(os.environ.get("KTOK", "1000")))
    B = Buf
    dram = lambda n, s, d: nc.dram_tensor(n, s, d, kind="Internal").ap()
    hsc = g.y
    hnTs = dram("hnTs", [g.NT4, 128, 16 * 128], BF16)
    Gs = g.x.bitcast(BF16).rearrange("(t f) c -> t (f c)", f=4)
    b_h = [B() for _ in range(NT4)]
    b_hn = [B() for _ in range(NT4)]
    b_G = [B() for _ in range(NT4)]
    xin_v = g.xin.rearrange("(k r i) f -> k i r f", r=4, i=256)

    with ExitStack() as st:
        sb = lambda n, s, d: st.enter_context(nc.sbuf_tensor(n, s, d))
        ps = lambda n, s, d: st.enter_context(nc.psum_tensor(n, s, d))
        Wo = sb("o_Wo", [128, 16, D], BF16)
        bWo = B()
        load_weight_bf16(g, sb, "o_w", g.wo, Wo, bWo)
        gso = sb("o_gso", [128, 16], F32)
        gn2 = sb("o_gn2", [128, 16], F32)
        bp = B()
        S.dma("sp", gso[:], g.gso, writes=[bp])
        S.dma("sp", gn2[:], g.gn2, writes=[bp])
        mixraw = sb("o_mixraw", [128, 4, 512], F32)
        mixq = [sb("o_mixq%d" % i, [128, 4, 512], F32) for i in range(2)]
        bmixq = [B(), B()]
        sel = sb("o_sel", [128, 4], F32)
        S.dma("sp", sel[:], g.sel.partition_broadcast(128), writes=[bp])
        mix = sb("o_mix", [128, D], F32)
        junk = sb("o_junk", [128, D], BF16)
        ssq = sb("o_ssq", [128, 8], F32)
        mixT = sb("o_mixT", [128, 16, 128], BF16)
        xt = sb("o_xt", [128, D], F32)
        h = sb("o_h", [128, D], F32)
        hs = sb("o_hs", [128, D], F32)
        hnT = sb("o_hnT", [128, 16, 128], BF16)
        bmr, bmix, bjunk, bssq, bmixT, bxt, bh, bhs, bhnT = B(), B(), B(), B(), B(), B(), B(), B(), B()
        pT = [ps("o_pT%d" % i, [128, 4, 128], F32) for i in range(2)]
        bpT = [PB(), PB()]
        pp = [ps("o_pp%d" % i, [128, 512], F32) for i in range(4)]
        bpp = [PB() for _ in range(4)]
        for ti in range(NT4):
            ts = slice(ti * 128, (ti + 1) * 128)
            for q in range(4):
                mq, bq_ = mixq[q % 2], bmixq[q % 2]
                s0 = q * T4 + ti * 128
                S.dma("sp", mq[:], xin_v[s0 // 256][s0 % 256:s0 % 256 + 128], reads=[g.b_xin[s0 // 256]], writes=[bq_])
                if q == 0:
                    S.op("dve", lambda e, mq=mq: e.tensor_scalar(out=mixraw[:], in0=mq[:], scalar1=sel[:, 0:1], scalar2=None,
                                                                 op0=ALU.mult), reads=[bq_, bp], writes=[bmr])
                else:
                    S.op("dve", lambda e, mq=mq, q=q: e.scalar_tensor_tensor(
                        out=mixraw[:].rearrange("p r f -> p (r f)"), in0=mq[:].rearrange("p r f -> p (r f)"),
                        scalar=sel[:, q:q + 1], in1=mixraw[:].rearrange("p r f -> p (r f)"), op0=ALU.mult, op1=ALU.add),
                        reads=[bq_, bp, bmr], writes=[bmr])
            S.dma("sp", xt[:], g.xq[ts, :], writes=[bxt])
            S.op("act", lambda e: e.activation(out=mix[:, 0:1024].rearrange("p (r f) -> p r f", r=4), in_=mixraw[:, :, 0:256],
                                               func=AF.Copy), reads=[bmr], writes=[bmix])
            S.op("dve", lambda e: e.tensor_copy(out=mix[:, 1024:2048].rearrange("p (r f) -> p r f", r=4),
                                                in_=mixraw[:, :, 256:512]), reads=[bmr], writes=[bmix])
            segs = [(0, 1024), (1024, 1536), (1536, 2048)]
            for si, (a, b_) in enumerate(segs):
                S.op("act", lambda e, si=si, a=a, b_=b_: e.activation(out=junk[:, a:b_], in_=mix[:, a:b_], func=AF.Square,
                                                                     accum_out=ssq[:, si:si + 1]),
                     reads=[bmix], writes=[bjunk, bssq])
            S.op("act", lambda e: e.activation(out=ssq[:, 4:5], in_=ssq[:, 0:1], func=AF.Sqrt, bias=EPS, scale=1.0 / 1024),
                 reads=[bssq], writes=[bssq])
            S.op("act", lambda e: e.activation(out=ssq[:, 5:7], in_=ssq[:, 1:3], func=AF.Sqrt, bias=EPS, scale=1.0 / 512),
                 reads=[bssq], writes=[bssq])
            S.op("dve", lambda e: e.reciprocal(out=ssq[:, 4:7], in_=ssq[:, 4:7]), reads=[bssq], writes=[bssq])
            for si, (a, b_) in enumerate(segs):
                S.op("act", lambda e, si=si, a=a, b_=b_: e.activation(out=mix[:, a:b_], in_=mix[:, a:b_], func=AF.Copy,
                                                                     scale=ssq[:, 4 + si:5 + si]),
                     reads=[bmix, bssq], writes=[bmix])
            norm_transpose(g, mix, bmix, gso, bp, mixT, bmixT, None, None, pT, bpT)
            for n in range(4):
                for kc in range(16):
                    S.op("pe", lambda e, n=n, kc=kc: e.matmul(pp[n][:], lhsT=mixT[:, kc, :], rhs=Wo[:, kc, n * 512:(n + 1) * 512],
                                                              start=(kc == 0), stop=(kc == 15)),
                         reads=[bmixT, bWo], writes=[bpp[n]])
                S.op("dve", lambda e, n=n: e.tensor_tensor(out=h[:, n * 512:(n + 1) * 512], in0=pp[n][:],
                                                           in1=xt[:, n * 512:(n + 1) * 512], op=ALU.add),
                     reads=[bpp[n], bxt], writes=[bh])
            S.dma("pool", hsc[ts, :], h[:], reads=[bh], writes=[b_h[ti]])
            S.op("act", lambda e: e.activation(out=junk[:], in_=h[:], func=AF.Square, accum_out=ssq[:, 3:4]),
                 reads=[bh], writes=[bjunk, bssq])
            rstd_from_ssq(S, ssq[:, 3:4], ssq[:, 7:8], D, bssq, bssq)
            S.op("act", lambda e: e.activation(out=hs[:], in_=h[:], func=AF.Copy, scale=ssq[:, 7:8]),
                 reads=[bh, bssq], writes=[bhs])
            norm_transpose(g, hs, bhs, gn2, bp, hnT, bhnT, None, None, pT, bpT)
            S.dma("pool", hnTs[ti].rearrange("p (c t) -> p c t", c=16), hnT[:], reads=[bhnT], writes=[b_hn[ti]])
        if hasattr(g, "dbg_h"):
            S.dma("sp", g.dbg_h, hsc, reads=b_h, writes=[g.b_dbg])
        S.emit()

    with ExitStack() as st:
        sb = lambda n, s, d: st.enter_context(nc.sbuf_tensor(n, s, d))
        ps = lambda n, s, d: st.enter_context(nc.psum_tensor(n, s, d))
        Wq = sb("r_Wq", [128, 16, D], BF16)
        bWq = B()
        load_weight_bf16(g, sb, "r_w", g.wq, Wq, bWq)
        keysT = sb("r_keysT", [128, 16, 128], F32)
        bk = B()
        S.dma("sp", keysT[:], g.keysT.rearrange("p (c i) -> p c i", c=16), writes=[bk])
        hnT = sb("r_hnT", [128, 16, 128], BF16)
        qTf = sb("r_qTf", [128, 16, 128], F32)
        sc = sb("r_sc", [128, 8, 2, 128], F32)
        tmp = sb("r_tmp", [128, 256], F32)
        v16 = sb("r_v16", [128, 8, 2, 16], F32)
        cand = sb("r_cand", [128, 8, 16, 16], F32)
        best = sb("r_best", [128, 8, 16], F32)
        sm = sb("r_sm", [128, 8, 8], F32)
        off1 = sb("r_off1", [128, 8, 128], F32)
        thr = sb("r_thr", [128, 8, 128], F32)
        up_ = sb("r_up", [128, 4, 8, 128], F32)
        ex_ = up_
        mk_ = sb("r_mk", [128, 4, 8, 128], F32)
        Gt = sb("r_Gt", [128, 128, 128], BF16)
        Gf = sb("r_Gf", [128, 4, 128], F32)
        bhnT, bqTf, bsc, btmp, bv16, bcand, bbest, bsm = B(), B(), B(), B(), B(), B(), B(), B()
        boff, bup, bex, bmk, bGt, bGf = B(), B(), B(), B(), B(), B()
        pq = [ps("r_pq%d" % i, [128, 4, 128], F32) for i in range(4)]
        bpq = [PB() for _ in range(4)]
        psc = [ps("r_psc%d" % i, [128, 4, 128], F32) for i in range(4)]
        bpsc = [PB() for _ in range(4)]
        for ti in range(NT4):
            S.dma("sp", hnT[:], hnTs[ti].rearrange("p (c t) -> p c t", c=16), reads=[b_hn[ti]], writes=[bhnT])
            for hc in range(16):
                for kc in range(16):
                    S.op("pe", lambda e, hc=hc, kc=kc: e.matmul(pq[hc // 4][:, hc % 4, :], lhsT=Wq[:, kc, hc * 128:(hc + 1) * 128],
                                                                rhs=hnT[:, kc, :], start=(kc == 0), stop=(kc == 15)),
                         reads=[bWq, bhnT], writes=[bpq[hc // 4]])
                if hc % 4 == 3:
                    S.op("act", lambda e, hc=hc: e.activation(out=qTf[:, hc - 3:hc + 1, :], in_=pq[hc // 4][:], func=AF.Copy),
                         reads=[bpq[hc // 4]], writes=[bqTf])
            scv = sc[:].rearrange("p h c i -> p (h c) i")
            for hc in range(16):
                S.op("pe", lambda e, hc=hc: e.matmul(psc[hc // 4][:, hc % 4, :], lhsT=qTf[:, hc, :], rhs=keysT[:, hc, :],
                                                     start=True, stop=True), reads=[bqTf, bk], writes=[bpsc[hc // 4]])
                if hc % 4 == 3:
                    S.op("act", lambda e, hc=hc: e.activation(out=scv[:, hc - 3:hc + 1, :], in_=psc[hc // 4][:], func=AF.Copy),
                         reads=[bpsc[hc // 4]], writes=[bsc])
            v16v = v16[:].rearrange("p h c k -> p (h c) k")
            for hc in range(16):
                S.op("dve", lambda e, hc=hc: e.max(out=v16v[:, hc, 0:8], in_=scv[:, hc, :]), reads=[bsc], writes=[bv16])
                S.op("dve", lambda e, hc=hc: e.match_replace(out=tmp[:, 0:128], in_to_replace=v16v[:, hc, 0:8],
                                                             in_values=scv[:, hc, :], imm_value=NEG),
                     reads=[bsc, bv16], writes=[btmp])
                S.op("dve", lambda e, hc=hc: e.max(out=v16v[:, hc, 8:16], in_=tmp[:, 0:128]), reads=[btmp], writes=[bv16])
            S.op("dve", lambda e: e.tensor_tensor(out=cand[:], in0=v16[:, :, 0, :].unsqueeze(3).to_broadcast([128, 8, 16, 16]),
                                                  in1=v16[:, :, 1, :].unsqueeze(2).to_broadcast([128, 8, 16, 16]), op=ALU.add),
                 reads=[bv16], writes=[bcand])
            for hh in range(8):
                cv = cand[:, hh].rearrange("p a b -> p (a b)")
                S.op("dve", lambda e, hh=hh, cv=cv: e.max(out=best[:, hh, 0:8], in_=cv), reads=[bcand], writes=[bbest])
                S.op("dve", lambda e, hh=hh, cv=cv: e.match_replace(out=tmp[:], in_to_replace=best[:, hh, 0:8], in_values=cv,
                                                                    imm_value=NEG), reads=[bcand, bbest], writes=[btmp])
                S.op("dve", lambda e, hh=hh: e.max(out=best[:, hh, 8:16], in_=tmp[:]), reads=[btmp], writes=[bbest])
            S.op("dve", lambda e: e.tensor_tensor(out=cand[:, :, 0, :], in0=best[:], in1=best[:, :, 0:1].to_broadcast([128, 8, 16]),
                                                  op=ALU.subtract), reads=[bbest, bcand], writes=[bcand])
            S.op("act", lambda e: e.activation(out=cand[:, :, 0, :], in_=cand[:, :, 0, :], func=AF.Exp), reads=[bcand], writes=[bcand])
            S.op("dve", lambda e: e.tensor_reduce(out=sm[:, :, 0], in_=cand[:, :, 0, :], axis=AX.X, op=ALU.add),
                 reads=[bcand], writes=[bsm])
            S.op("act", lambda e: e.activation(out=sm[:, :, 1], in_=sm[:, :, 0], func=AF.Ln), reads=[bsm], writes=[bsm])
            S.op("dve", lambda e: e.tensor_tensor(out=sm[:, :, 2], in0=sm[:, :, 1], in1=best[:, :, 0], op=ALU.add),
                 reads=[bsm, bbest], writes=[bsm])
            S.op("dve", lambda e: e.tensor_tensor(out=sm[:, :, 3], in0=best[:, :, 15], in1=sm[:, :, 2], op=ALU.subtract),
                 reads=[bsm, bbest], writes=[bsm])
            S.op("act", lambda e: e.activation(out=sm[:, :, 4], in_=sm[:, :, 3], func=AF.Exp), reads=[bsm], writes=[bsm])
            S.op("dve", lambda e: e.tensor_tensor(out=thr[:], in0=best[:, :, 15:16].to_broadcast([128, 8, 128]), in1=sc[:, :, 0, :],
                                                  op=ALU.subtract), reads=[bsc, bbest], writes=[boff])
            s2b = sc[:, :, 1, :].unsqueeze(1).to_broadcast([128, 4, 8, 128])
            for ib in range(32):
                isl = slice(ib * 4, ib * 4 + 4)
                t1 = thr[:, :, isl].rearrange("p h i -> p i h").unsqueeze(3).to_broadcast([128, 4, 8, 128])
                S.op("pool", lambda e, t1=t1: e.tensor_tensor(out=up_[:], in0=s2b, in1=t1, op=ALU.subtract),
                     reads=[bsc, boff], writes=[bup])
                S.op("act", lambda e: e.activation(out=mk_[:], in_=up_[:], func=AF.Exp), reads=[bup], writes=[bmk])
                S.op("dve", lambda e: e.scalar_tensor_tensor(
                    out=mk_[:].rearrange("p i h j -> p (i h j)"), in0=up_[:].rearrange("p i h j -> p (i h j)"), scalar=0.0,
                    in1=mk_[:].rearrange("p i h j -> p (i h j)"), op0=ALU.is_ge, op1=ALU.mult), reads=[bup, bmk], writes=[bmk])
                S.op("dve", lambda e: e.tensor_scalar(out=Gf[:], in0=mk_[:, :, 0, :], scalar1=sm[:, 0, 4:5], scalar2=None,
                                                      op0=ALU.mult), reads=[bmk, bsm], writes=[bGf])
                for hh in range(1, 8):
                    if hh < 7:
                        S.op("dve", lambda e, hh=hh: e.scalar_tensor_tensor(
                            out=Gf[:], in0=mk_[:, :, hh, :], scalar=sm[:, hh, 4:5], in1=Gf[:], op0=ALU.mult, op1=ALU.add),
                            reads=[bmk, bsm, bGf], writes=[bGf])
                    else:
                        S.op("dve", lambda e, hh=hh, isl=isl: e.scalar_tensor_tensor(
                            out=Gt[:, isl, :], in0=mk_[:, :, hh, :], scalar=sm[:, hh, 4:5], in1=Gf[:], op0=ALU.mult, op1=ALU.add),
                            reads=[bmk, bsm, bGf], writes=[bGt])
            S.dma("pool", Gs[ti * 128:(ti + 1) * 128, :], Gt[:].rearrange("p a b -> p (a b)"), reads=[bGt], writes=[b_G[ti]])
        if hasattr(g, "dbg_G"):
            pass
        S.emit()

    with ExitStack() as st:
        sb = lambda n, s, d: st.enter_context(nc.sbuf_tensor(n, s, d))
        ps = lambda n, s, d: st.enter_context(nc.psum_tensor(n, s, d))
        NTG = min(2, NT4)
        hn2 = sb("f_hn", [128, NTG, 16, 128], BF16)
        bhn2 = B()
        HT = sb("f_HT", [128, 128, NTG, 128], BF16)
        bHT = B()
        dT = [sb("f_dT%d" % i, [128, 16, 512], BF16) for i in range(2)]
        bdT = [B(), B()]
        Gc = [sb("f_Gc%d" % i, [128, NTG, 512], BF16) for i in range(2)]
        bGc = [B(), B()]
        ge = [sb("f_ge%d" % i, [128, 512], F32) for i in range(2)]
        bge = [B(), B()]
        Hb = [sb("f_Hb%d" % i, [128, 512], BF16) for i in range(2)]
        bHb = [B(), B()]
        upb = [sb("f_up%d" % i, [128, D], BF16) for i in range(4)]
        bupb = [B() for _ in range(4)]
        hres = sb("f_hres", [128, D], F32)
        bhres = B()
        outt = [sb("f_out%d" % i, [128, D], F32) for i in range(2)]
        bout = [B(), B()]
        bank = [ps("f_bk%d" % i, [128, 512], F32) for i in range(8)]
        bbank = [PB() for _ in range(8)]
        pTb = [bank[4][:].bitcast(BF16), bank[5][:].bitcast(BF16)]
        k = 0
        for gi in range(0, NT4, NTG):
            for j in range(NTG):
                S.dma("sp", hn2[:, j], hnTs[gi + j].rearrange("p (c t) -> p c t", c=16), reads=[b_hn[gi + j]], writes=[bhn2])
            for e4 in range(NE // 512):
                i = e4 % 2
                S.dma("sp", dT[i][:], g.dTs[e4].rearrange("p (c e) -> p c e", c=16), reads=[g.b_dTs[e4]], writes=[bdT[i]])
                S.dma("sp", Gc[i][:], Gs[gi * 128:(gi + NTG) * 128, e4 * 512:(e4 + 1) * 512].rearrange("(j p) e -> p j e", p=128),
                      reads=b_G[gi:gi + NTG], writes=[bGc[i]])
                for j in range(NTG):
                    a = k % 2
                    k += 1
                    for kc in range(16):
                        S.op("pe", lambda e, a=a, j=j, kc=kc, i=i: e.matmul(bank[a][:], lhsT=hn2[:, j, kc, :], rhs=dT[i][:, kc, :],
                                                                            start=(kc == 0), stop=(kc == 15)),
                             reads=[bhn2, bdT[i]], writes=[bbank[a]])
                    S.op("act", lambda e, a=a: e.activation(out=ge[a][:], in_=bank[a][:], func=AF.Gelu),
                         reads=[bbank[a]], writes=[bge[a]])
                    S.op("dve", lambda e, a=a, i=i, j=j: e.tensor_tensor(out=Hb[a][:], in0=ge[a][:], in1=Gc[i][:, j, :], op=ALU.mult),
                         reads=[bge[a], bGc[i]], writes=[bHb[a]])
                    for c in range(4):
                        S.op("pe", lambda e, a=a, c=c: e.transpose(out=pTb[a][:, c * 128:(c + 1) * 128],
                                                                   in_=Hb[a][:, c * 128:(c + 1) * 128], identity=g.ident_b[:]),
                             reads=[bHb[a], g.bc], writes=[bbank[4 + a]])
                    S.op("act", lambda e, a=a, e4=e4, j=j: e.activation(
                        out=HT[:, e4 * 4:(e4 + 1) * 4, j, :], in_=pTb[a][:, 0:512].rearrange("p (c t) -> p c t", c=4), func=AF.Copy),
                        reads=[bbank[4 + a]], writes=[bHT])
            for eb in range(NE // 128):
                u = eb % 4
                S.dma("pool", upb[u][:], g.ups[eb], reads=[g.b_ups[eb]], writes=[bupb[u]])
                for j in range(NTG):
                    for n in range(4):
                        S.op("pe", lambda e, j=j, n=n, eb=eb, u=u: e.matmul(bank[j * 4 + n][:], lhsT=HT[:, eb, j, :],
                                                                            rhs=upb[u][:, n * 512:(n + 1) * 512],
                                                                            start=(eb == 0), stop=(eb == NE // 128 - 1)),
                             reads=[bHT, bupb[u]], writes=[bbank[j * 4 + n]])
            for j in range(NTG):
                ts = slice((gi + j) * 128, (gi + j + 1) * 128)
                S.dma("sp", hres[:], hsc[ts, :], reads=[b_h[gi + j]], writes=[bhres])
                for n in range(4):
                    S.op("dve", lambda e, j=j, n=n: e.tensor_tensor(out=outt[j][:, n * 512:(n + 1) * 512], in0=bank[j * 4 + n][:],
                                                                   in1=hres[:, n * 512:(n + 1) * 512], op=ALU.add),
                         reads=[bbank[j * 4 + n], bhres], writes=[bout[j]])
                S.dma("pool", g.y[ts, :], outt[j][:], reads=[bout[j]], writes=[g.b_y, b_h[gi + j]])


def kernel(x, attn_norm, w_in, sb_q_norm, sb_k_norm, conv_w, conv_b, dt_bias, a_log, d_skip,
           sb_out_norm, ssd_out_norm, w_out, ffn_norm, peer_query, peer_sub_keys, peer_down, peer_up,
           _debug=False):
    x = np.asarray(x, np.float32)
    Bsz, SEQ, _ = x.shape
    assert Bsz == 2 and SEQ % 512 == 0
    T4 = SEQ // 4
    f = lambda a: np.ascontiguousarray(np.asarray(a, np.float32))
    w_in0 = np.asarray(w_in, np.float32)[0]
    pc = lambda v: f(np.asarray(v, np.float32).reshape(16, 128).T)
    conv_w0 = np.asarray(conv_w, np.float32)[0, :, 0, :]
    conv_b0 = np.asarray(conv_b, np.float32)[0]
    keysT = f(np.asarray(peer_sub_keys, np.float32)[0].reshape(16, 128, 128).transpose(2, 0, 1).reshape(128, 16 * 128))
    gso = pc(np.concatenate([np.asarray(sb_out_norm, np.float32)[0], np.asarray(ssd_out_norm, np.float32)[0]]))
    shared = {
        "gn1": pc(attn_norm[0]), "gso": gso, "wo": f(w_out[0]), "gn2": pc(ffn_norm[0]), "wq": f(peer_query[0]),
        "keysT": keysT,
        "down": f(peer_down[0]) if "t" in os.environ.get("KPH", "tpacx") else f(peer_down[0][:128]),
        "up": f(peer_up[0]) if "t" in os.environ.get("KPH", "tpacx") else f(peer_up[0][:128]),
        "gqk": f(np.stack([np.asarray(sb_q_norm, np.float32)[0], np.asarray(sb_k_norm, np.float32)[0]], axis=1)),
    }
    in_maps = []
    for c in range(8):
        b, hq = c // 4, c % 4
        grp = hq // 2
        cols = np.concatenate([
            np.arange(256 * hq, 256 * hq + 256),
            1024 + np.arange(256 * hq, 256 * hq + 256),
            4096 + np.arange(256 * hq, 256 * hq + 256),
            4096 + 1024 + np.arange(128 * grp, 128 * grp + 128),
            4096 + 1024 + 256 + np.arange(128 * grp, 128 * grp + 128),
            2048 + np.arange(256 * hq, 256 * hq + 256),
            3072 + np.arange(256 * hq, 256 * hq + 256),
            4096 + 1536 + np.arange(4 * hq, 4 * hq + 4),
        ])
        cch = np.concatenate([np.arange(256 * hq, 256 * hq + 256), 1024 + np.arange(128 * grp, 128 * grp + 128),
                              1024 + 256 + np.arange(128 * grp, 128 * grp + 128)])
        cwc = conv_w0[:, cch]
        ssdp = np.zeros((1, 12), np.float32)
        ssdp[0, 0:4] = np.asarray(dt_bias, np.float32)[0, 4 * hq:4 * hq + 4]
        ssdp[0, 4:8] = np.asarray(a_log, np.float32)[0, 4 * hq:4 * hq + 4]
        m = dict(shared)
        m.update({
            "x": f(x[b]), "xq": f(x[b, hq * T4:(hq + 1) * T4]), "w1": f(w_in0[:, cols]),
            "cw": f(cwc.reshape(4, 4, 128).transpose(2, 1, 0).reshape(128, 16)),
            "cb": f(conv_b0[cch].reshape(4, 128).T), "ssdp": ssdp,
            "sel": f(np.eye(4, dtype=np.float32)[hq][None, :]),
            "dsk": f(np.repeat(np.asarray(d_skip, np.float32)[0, 4 * hq:4 * hq + 4], 64)[None, :]),
        })
        in_maps.append(m)
    nc = build_program(SEQ, debug=_debug)
    res = run_bass_kernel_spmd(nc, in_maps, core_ids=list(range(8)))
    out = np.empty((2, SEQ, D), np.float32)
    for c in range(8):
        b, hq = c // 4, c % 4
        out[b, hq * T4:(hq + 1) * T4] = res.results[c]["y"]
    if _debug:
        return out, res
    return out
```

```python
import os
import numpy as np
from contextlib import ExitStack
import concourse.bass as bass
import concourse.mybir as mybir
from concourse.bass_utils import run_bass_kernel_spmd

F32 = mybir.dt.float32
BF16 = mybir.dt.bfloat16
AF = mybir.ActivationFunctionType
ALU = mybir.AluOpType
AX = mybir.AxisListType

D = 2048
NE = 16384
EPS = 1e-6
NEG = -1.0e30


class Buf:
    __slots__ = ("w", "r", "x")

    def __init__(self, x=False):
        self.w = None
        self.r = []
        self.x = x


def PB():
    return Buf(True)


class Sched:
    NDMA = 8

    def __init__(self, nc, stack):
        self.nc = nc
        self.eng = {"pe": nc.tensor, "dve": nc.vector, "act": nc.scalar,
                    "pool": nc.gpsimd, "sp": nc.sync}
        self.semh = {}
        self.cnt = {}
        self.seen = {k: {} for k in self.eng}
        for k in self.eng:
            self.semh[k] = stack.enter_context(nc.semaphore("s_" + k))
            self.cnt[k] = 0
        self.dma_rr = {}
        for q in ("sp", "act", "pool"):
            self.dma_rr[q] = 0
            for i in range(self.NDMA):
                key = "d_%s_%d" % (q, i)
                self.semh[key] = stack.enter_context(nc.semaphore(key))
                self.cnt[key] = 0
        self.semh["cc"] = stack.enter_context(nc.semaphore("s_cc"))
        self.cnt["cc"] = 0
        self.prog = {k: [] for k in self.eng}
        self.n_ins = 0

    def _waits(self, e, deps):
        need = {}
        for (k, v) in deps:
            if v > need.get(k, 0):
                need[k] = v
        seen = self.seen[e]
        for k, v in need.items():
            if k == e and e == "pe":
                continue
            if seen.get(k, 0) >= v:
                continue
            self.prog[e].append((0, k, v))
            seen[k] = v

    @staticmethod
    def _deps(reads, writes):
        deps = []
        for b in reads:
            if b.w is not None:
                deps.append(b.w)
        for b in writes:
            deps.extend(b.r)
            if b.w is not None:
                deps.append(b.w)
        return deps

    @staticmethod
    def _record(ev, reads, writes):
        for b in reads:
            b.r.append(ev)
            if len(b.r) > 48:
                m = {}
                for (k, v) in b.r:
                    if v > m.get(k, 0):
                        m[k] = v
                b.r = list(m.items())
        for b in writes:
            b.w = ev
            b.r = []

    def op(self, e, fn, reads=(), writes=()):
        if any(b.x for b in reads):
            writes = list(writes) + [b for b in reads if b.x]
            reads = [b for b in reads if not b.x]
        self._waits(e, self._deps(reads, writes))
        self.cnt[e] += 1
        self.prog[e].append((1, fn, e, 1))
        self.n_ins += 1
        ev = (e, self.cnt[e])
        self._record(ev, reads, writes)
        return ev

    def dma(self, q, out, in_, reads=(), writes=()):
        i = self.dma_rr[q]
        self.dma_rr[q] = (i + 1) % self.NDMA
        key = "d_%s_%d" % (q, i)
        deps = self._deps(reads, writes)
        if self.cnt[key] > 0:
            deps.append((key, self.cnt[key]))
        self._waits(q, deps)
        self.cnt[key] += 16
        self.prog[q].append((1, (lambda eng, o=out, i_=in_: eng.dma_start(out=o, in_=i_)), key, 16))
        self.n_ins += 1
        ev = (key, self.cnt[key])
        self._record(ev, reads, writes)
        return ev

    def collective(self, fn, reads=(), writes=()):
        self._waits("pool", self._deps(reads, writes))
        self.cnt["cc"] += 1
        self.prog["pool"].append((1, fn, "cc", 1))
        ev = ("cc", self.cnt["cc"])
        self._record(ev, reads, writes)
        return ev

    def wait_all(self, e, bufs):
        deps = []
        for b in bufs:
            deps.extend(b.r)
            if b.w is not None:
                deps.append(b.w)
        self._waits(e, deps)

    def barrier(self):
        for e in self.eng:
            self._waits(e, [(k, v) for k, v in self.cnt.items() if v > 0 and not (k == e and e == "pe")])

    def emit(self):
        self.barrier()
        nc = self.nc
        semh = self.semh
        prog = self.prog
        self.prog = {k: [] for k in self.eng}

        def run(eng, lst):
            for it in lst:
                if it[0] == 0:
                    eng.wait_ge(semh[it[1]], it[2])
                else:
                    it[1](eng).then_inc(semh[it[2]], it[3])

        with nc.Block() as block:
            if prog["sp"]:
                block.sync(lambda eng: run(eng, prog["sp"]))
            if prog["pe"]:
                block.tensor(lambda eng: run(eng, prog["pe"]))
            if prog["dve"]:
                block.vector(lambda eng: run(eng, prog["dve"]))
            if prog["act"]:
                block.scalar(lambda eng: run(eng, prog["act"]))
            if prog["pool"]:
                block.gpsimd(lambda eng: run(eng, prog["pool"]))


class Ctx:
    pass


def build_program(SEQ, debug=False):
    NT = SEQ // 128
    NG = SEQ // 512
    T4 = SEQ // 4
    NT4 = T4 // 128
    nc = bass.Bass("TRN2", target_bir_lowering=False)
    dram = lambda n, s, d, k: nc.dram_tensor(n, s, d, kind=k).ap()
    g = Ctx()
    g.nc = nc
    g.SEQ, g.NT, g.NG, g.T4, g.NT4 = SEQ, NT, NG, T4, NT4
    g.x = dram("x", [SEQ, D], F32, "ExternalInput")
    g.xq = dram("xq", [T4, D], F32, "ExternalInput")
    g.w1 = dram("w1", [D, 1540], F32, "ExternalInput")
    g.gn1 = dram("gn1", [128, 16], F32, "ExternalInput")
    g.gqk = dram("gqk", [128, 2], F32, "ExternalInput")
    g.cw = dram("cw", [128, 16], F32, "ExternalInput")
    g.cb = dram("cb", [128, 4], F32, "ExternalInput")
    g.ssdp = dram("ssdp", [1, 12], F32, "ExternalInput")
    g.dsk = dram("dsk", [1, 256], F32, "ExternalInput")
    g.gso = dram("gso", [128, 16], F32, "ExternalInput")
    g.wo = dram("wo", [D, D], F32, "ExternalInput")
    g.gn2 = dram("gn2", [128, 16], F32, "ExternalInput")
    g.wq = dram("wq", [D, D], F32, "ExternalInput")
    g.keysT = dram("keysT", [128, 16 * 128], F32, "ExternalInput")
    big = "t" in os.environ.get("KPH", "tpacx")
    g.down = dram("down", [NE if big else 128, D], F32, "ExternalInput")
    g.up = dram("up", [NE if big else 128, D], F32, "ExternalInput")
    g.y = dram("y", [T4, D], F32, "ExternalOutput")
    g.qTs = dram("qTs", [2, 128, SEQ], BF16, "Internal")
    g.kTs = dram("kTs", [2, 128, SEQ], BF16, "Internal")
    g.vs = dram("vs", [SEQ, 256], BF16, "Internal")
    g.xch = dram("xch", [SEQ, 512], F32, "Internal")
    g.xin = dram("xin", [4 * SEQ, 512], F32, "Internal")
    g.sel = dram("sel", [1, 4], F32, "ExternalInput")
    if big:
        g.dTs = g.down.bitcast(BF16).rearrange("(e r) c -> e (r c)", r=512)[:, 0:128 * 8192].rearrange("e (p k) -> e p k", p=128)
        g.ups = g.up.bitcast(BF16).rearrange("(e r) c -> e (r c)", r=128)[:, 0:128 * D].rearrange("e (p k) -> e p k", p=128)
    else:
        g.dTs = dram("dTs", [NE // 512, 128, 16 * 512], BF16, "Internal")
        g.ups = dram("ups", [NE // 128, 128, D], BF16, "Internal")
    if debug:
        g.dbg_h = dram("dbg_h", [T4, D], F32, "ExternalOutput")
        g.dbg_x = dram("dbg_x", [SEQ, 512], F32, "ExternalOutput")
    g.b_qTs, g.b_kTs, g.b_vs, g.b_xch = Buf(), Buf(), Buf(), Buf()
    g.b_dTs, g.b_ups, g.b_y = Buf(), Buf(), Buf()
    g.b_dbg = Buf()

    ph = os.environ.get("KPH", "tpacx")
    g.ph = ph
    with ExitStack() as st0:
        S = Sched(nc, st0)
        g.S = S
        consts(g, st0)
        S.emit()
        g.b_dTs = [Buf() for _ in range(NE // 512)]
        g.b_ups = [Buf() for _ in range(NE // 128)]
        g.b_q = [Buf() for _ in range(NG)]
        g.b_k = [Buf() for _ in range(NG)]
        g.b_v = [Buf() for _ in range(NT)]
        g.b_xs = [Buf() for _ in range(NT)]
        g.b_xa = [Buf() for _ in range(NT)]
        g.b_xin = [Buf() for _ in range(SEQ // 256)]
        if "t" in ph:
            with ExitStack() as st:
                phase_tables(g, st)
                S.emit()
        if "p" in ph:
            with ExitStack() as st:
                phase_proj_ssd(g, st)
                S.emit()
        if "a" in ph:
            with ExitStack() as st:
                phase_attn(g, st)
                S.emit()
        if debug and "a" in ph:
            S.dma("sp", g.dbg_x[:, 0:256], g.xch[:, 0:256], reads=g.b_xa, writes=[g.b_dbg])
        if debug and "p" in ph:
            S.dma("sp", g.dbg_x[:, 256:512], g.xch[:, 256:512], reads=g.b_xs, writes=[g.b_dbg])
        S.emit()
        with ExitStack() as st:
            if "x" in ph:
                phase_tokens(g, st)
            S.wait_all("sp", [g.b_y, g.b_dbg] + g.b_xin + g.b_dTs + g.b_ups)
            S.emit()
    return nc


def consts(g, st):
    nc, S = g.nc, g.S
    sb = lambda n, s, d: st.enter_context(nc.sbuf_tensor(n, s, d))
    g.ones_f = sb("ones_f", [128, 128], F32)
    g.ones_b = sb("ones_b", [128, 128], BF16)
    g.ident_f = sb("ident_f", [128, 128], F32)
    g.ident_b = sb("ident_b", [128, 128], BF16)
    g.tri_f = sb("tri_f", [128, 128], F32)
    g.trl_b = sb("trl_b", [128, 128], BF16)
    g.cm_f = sb("cm_f", [128, 4, 512], F32)
    tmp = sb("c_tmp", [128, 128], F32)
    b = g.bc = Buf()
    bt = Buf()
    S.op("pool", lambda e: e.memset(g.ones_f[:], 1.0), writes=[b])
    S.op("pool", lambda e: e.memset(g.ones_b[:], 1.0), writes=[b])
    S.op("pool", lambda e: e.affine_select(out=g.ident_f[:], in_=g.ones_f[:], pattern=[[-1, 128]],
                                           compare_op=ALU.is_equal, fill=0.0, base=0, channel_multiplier=1),
         reads=[b], writes=[b])
    S.op("pool", lambda e: e.tensor_copy(out=g.ident_b[:], in_=g.ident_f[:]), reads=[b], writes=[b])
    S.op("pool", lambda e: e.affine_select(out=g.tri_f[:], in_=g.ones_f[:], pattern=[[1, 128]],
                                           compare_op=ALU.is_ge, fill=0.0, base=0, channel_multiplier=-1),
         reads=[b], writes=[b])
    S.op("pool", lambda e: e.affine_select(out=tmp[:], in_=g.ones_f[:], pattern=[[-1, 128]],
                                           compare_op=ALU.is_ge, fill=0.0, base=-1, channel_multiplier=1),
         reads=[b], writes=[bt])
    S.op("pool", lambda e: e.tensor_copy(out=g.trl_b[:], in_=tmp[:]), reads=[bt], writes=[b])
    for d in range(4):
        S.op("pool", lambda e, d=d: e.memset(g.cm_f[:, d, :], 1.0), writes=[b])
        S.op("pool", lambda e, d=d: e.affine_select(out=g.cm_f[:, d, :], in_=g.cm_f[:, d, :], pattern=[[1, 512]],
                                                    compare_op=ALU.is_ge, fill=0.0, base=-(d * 128) - 1,
                                                    channel_multiplier=-1), reads=[b], writes=[b])


def rstd_from_ssq(S, ssq, rstd, n, bs, br):
    S.op("act", lambda e: e.activation(out=rstd, in_=ssq, func=AF.Sqrt, bias=EPS, scale=1.0 / n),
         reads=[bs], writes=[br])
    S.op("dve", lambda e: e.reciprocal(out=rstd, in_=rstd), reads=[br], writes=[br])


def phase_tables(g, st):
    nc, S = g.nc, g.S
    sb = lambda n, s, d: st.enter_context(nc.sbuf_tensor(n, s, d))
    ps = lambda n, s, d: st.enter_context(nc.psum_tensor(n, s, d))
    ld = [sb("t_ld%d" % i, [128, D], F32) for i in range(2)]
    ldu = [sb("t_ldu%d" % i, [128, D], F32) for i in range(2)]
    ucv = [sb("t_ucv%d" % i, [128, D], BF16) for i in range(2)]
    stg = [sb("t_stg%d" % i, [128, 16, 512], BF16) for i in range(2)]
    pT = [ps("t_pT%d" % i, [128, 4, 128], F32) for i in range(4)]
    bld, bldu, bucv, bstg = [Buf(), Buf()], [Buf(), Buf()], [Buf(), Buf()], [Buf(), Buf()]
    bpT = [PB() for _ in range(4)]
    for eb in range(NE // 128):
        i = eb % 2
        s = (eb // 4) % 2
        S.dma("sp", ld[i][:], g.down[eb * 128:(eb + 1) * 128, :], writes=[bld[i]])
        for r in range(4):
            for c in range(4):
                kc = r * 4 + c
                S.op("pe", lambda e, r=r, c=c, kc=kc, i=i: e.transpose(
                    out=pT[r][:, c, :], in_=ld[i][:, kc * 128:(kc + 1) * 128], identity=g.ident_f[:]),
                    reads=[bld[i], g.bc], writes=[bpT[r]])
            dst = stg[s][:, r * 4:(r + 1) * 4, (eb % 4) * 128:(eb % 4 + 1) * 128]
            if r % 2 == 0:
                S.op("act", lambda e, r=r, dst=dst: e.activation(out=dst, in_=pT[r][:], func=AF.Copy),
                     reads=[bpT[r]], writes=[bstg[s]])
            else:
                S.op("dve", lambda e, r=r, dst=dst: e.tensor_copy(out=dst, in_=pT[r][:]),
                     reads=[bpT[r]], writes=[bstg[s]])
        if eb % 4 == 3:
            S.dma("pool", g.dTs[eb // 4].rearrange("p (c e) -> p c e", c=16), stg[s][:],
                  reads=[bstg[s]], writes=[g.b_dTs[eb // 4]])
        S.dma("sp", ldu[i][:], g.up[eb * 128:(eb + 1) * 128, :], writes=[bldu[i]])
        S.op("pool", lambda e, i=i: e.tensor_copy(out=ucv[i][:], in_=ldu[i][:]), reads=[bldu[i]], writes=[bucv[i]])
        S.dma("pool", g.ups[eb], ucv[i][:], reads=[bucv[i]], writes=[g.b_ups[eb]])


def phase_proj_ssd(g, st):
    nc, S, NG = g.nc, g.S, g.NG
    sb = lambda n, s, d: st.enter_context(nc.sbuf_tensor(n, s, d))
    ps = lambda n, s, d: st.enter_context(nc.psum_tensor(n, s, d))
    B = Buf
    Wb = sb("p1_Wb", [128, 16, 1540], BF16)
    bWb = B()
    wst = [sb("p1_wst%d" % i, [128, 1540], F32) for i in range(2)]
    bwst = [B(), B()]
    for kc in range(16):
        i = kc % 2
        S.dma("sp", wst[i][:], g.w1[kc * 128:(kc + 1) * 128, :], writes=[bwst[i]])
        S.op("dve", lambda e, kc=kc, i=i: e.tensor_copy(out=Wb[:, kc, :], in_=wst[i][:]), reads=[bwst[i]], writes=[bWb])
    gn1 = sb("p1_gn1", [128, 16], F32)
    gqk = sb("p1_gqk", [128, 2], F32)
    cw = sb("p1_cw", [128, 4, 4], F32)
    cb = sb("p1_cb", [128, 4], F32)
    ssdp = sb("p1_ssdp", [128, 12], F32)
    aneg = sb("p1_aneg", [128, 4], F32)
    dsk = sb("p1_dsk", [128, 256], F32)
    bp = B()
    S.dma("sp", gn1[:], g.gn1, writes=[bp])
    S.dma("sp", gqk[:], g.gqk, writes=[bp])
    S.dma("sp", cw[:], g.cw.rearrange("p (c j) -> p c j", c=4), writes=[bp])
    S.dma("sp", cb[:], g.cb, writes=[bp])
    S.dma("sp", ssdp[:], g.ssdp.partition_broadcast(128), writes=[bp])
    S.dma("sp", dsk[:], g.dsk.partition_broadcast(128), writes=[bp])
    S.op("dve", lambda e: e.tensor_scalar(out=gqk[:, 0:1], in0=gqk[:, 0:1], scalar1=float(128.0 ** -0.5), scalar2=None,
                                          op0=ALU.mult), reads=[bp], writes=[bp])
    S.op("act", lambda e: e.activation(out=aneg[:], in_=ssdp[:, 4:8], func=AF.Exp), reads=[bp], writes=[bp])
    S.op("dve", lambda e: e.tensor_scalar(out=aneg[:], in0=aneg[:], scalar1=-1.0, scalar2=None, op0=ALU.mult),
         reads=[bp], writes=[bp])
    dtb = ssdp[:, 0:4]
    prevT = sb("p1_prevT", [128, 4, 64], F32)
    prev_b = sb("p1_prevb", [128, 4, 64], BF16)
    bprev, bprevb = B(), B()
    S.op("pool", lambda e: e.memset(prevT[:], 0.0), writes=[bprev])
    S.op("pool", lambda e: e.memset(prev_b[:], 0.0), writes=[bprevb])
    cin = [sb("p1_cin%d" % i, [128, 515], F32) for i in range(4)]
    bcin = [B() for _ in range(4)]
    for i in range(4):
        S.op("pool", lambda e, i=i: e.memset(cin[i][:, 0:3], 0.0), writes=[bcin[i]])
    xld = [sb("p1_xld%d" % i, [128, D], F32) for i in range(2)]
    bxld = [B(), B()]
    junk = sb("p1_junk", [128, D], BF16)
    bjunk = B()
    xs = sb("p1_xs", [128, D], F32)
    bxs = B()
    ssq = sb("p1_ssq", [128, 2], F32)
    bssq = B()
    xnT = [sb("p1_xnT%d" % i, [128, 16, 512], BF16) for i in range(2)]
    bxnT = [B(), B()]
    sqf = sb("p1_sqf", [128, 512], F32)
    bsqf = B()
    rsb = sb("p1_rsb", [128, 512], F32)
    brsb = B()
    qko = [sb("p1_qko%d" % i, [128, 512], BF16) for i in range(2)]
    bqko = [B(), B()]
    cacc = sb("p1_cacc", [128, 512], F32)
    bcacc = B()
    xcf = [sb("p1_xcf%d" % i, [128, 512], F32) for i in range(2)]
    Bcf = sb("p1_Bcf", [128, 512], F32)
    Bc = sb("p1_Bc", [128, 512], BF16)
    Cc = sb("p1_Cc", [128, 512], BF16)
    bxc = B()
    vb = [sb("p1_vb%d" % i, [128, 256], BF16) for i in range(2)]
    bvb = [B(), B()]
    zs = sb("p1_zs", [128, 256], F32)
    bzs = B()
    dt4 = sb("p1_dt4", [128, 4], F32)
    adt = sb("p1_adt", [128, 4], F32)
    acol = sb("p1_acol", [128, 4], F32)
    w4 = sb("p1_w4", [128, 4], F32)
    bsm = B()
    adt_rep = sb("p1_adtrep", [128, 4, 128], F32)
    badr = B()
    xtm = sb("p1_xtm", [128, 256], F32)
    Btm = sb("p1_Btm", [128, 128], BF16)
    xdt = sb("p1_xdt", [128, 4, 64], BF16)
    xw = sb("p1_xw", [128, 4, 64], BF16)
    bxtm, bBtm, bxdt, bxw = B(), B(), B(), B()
    cbm = sb("p1_cbm", [128, 128], F32)
    bcbm = B()
    arg = sb("p1_arg", [128, 4, 128], F32)
    barg = B()
    Mb = sb("p1_Mb", [128, 4, 128], BF16)
    bMb = B()
    eacs = sb("p1_eacs", [128, 4, 128], F32)
    beacs = B()
    Cdec = sb("p1_Cdec", [128, 4, 128], BF16)
    bCdec = B()
    yo = [sb("p1_yo%d" % i, [128, 256], F32) for i in range(2)]
    byo = [B(), B()]
    yt = sb("p1_yt", [128, 256], F32)
    byt = B()
    pT = [ps("p1_pT%d" % i, [128, 4, 128], F32) for i in range(2)]
    bpT = [PB(), PB()]
    pf = ps("p1_pf", [128, 512], F32)
    bpf = PB()
    psm = ps("p1_psm", [128, 512], F32)
    bpsm = PB()
    pv = ps("p1_pv", [128, 512], F32)
    bpv = PB()
    pm = ps("p1_pm", [128, 512], F32)
    bpm = PB()
    pacs = ps("p1_pacs", [128, 4, 128], F32)
    bpacs = PB()
    py = ps("p1_py", [128, 512], F32)
    bpy = PB()

    for gi in range(NG):
        xT = xnT[gi % 2]
        bxT = bxnT[gi % 2]
        for j in range(4):
            t = gi * 4 + j
            xt, bx = xld[t % 2], bxld[t % 2]
            S.dma("sp", xt[:], g.x[t * 128:(t + 1) * 128, :], writes=[bx])
            S.op("act", lambda e, xt=xt: e.activation(out=junk[:], in_=xt[:], func=AF.Square, accum_out=ssq[:, 0:1]),
                 reads=[bx], writes=[bjunk, bssq])
            rstd_from_ssq(S, ssq[:, 0:1], ssq[:, 1:2], D, bssq, bssq)
            S.op("act", lambda e, xt=xt: e.activation(out=xs[:], in_=xt[:], func=AF.Copy, scale=ssq[:, 1:2]),
                 reads=[bx, bssq], writes=[bxs])
            for r in range(4):
                p_, bp_ = pT[r % 2], bpT[r % 2]
                for c in range(4):
                    kc = r * 4 + c
                    S.op("pe", lambda e, p_=p_, c=c, kc=kc: e.transpose(
                        out=p_[:, c, :], in_=xs[:, kc * 128:(kc + 1) * 128], identity=g.ident_f[:]),
                        reads=[bxs, g.bc], writes=[bp_])
                S.op("dve", lambda e, p_=p_, r=r, j=j, xT=xT: e.tensor_tensor(
                    out=xT[:, r * 4:(r + 1) * 4, j * 128:(j + 1) * 128], in0=p_[:],
                    in1=gn1[:, r * 4:(r + 1) * 4].unsqueeze(2).to_broadcast([128, 4, 128]), op=ALU.mult),
                    reads=[bp_, bp], writes=[bxT])
        for oc in range(8):
            for kc in range(16):
                S.op("pe", lambda e, kc=kc, oc=oc, xT=xT: e.matmul(
                    pf[:], lhsT=Wb[:, kc, oc * 128:(oc + 1) * 128], rhs=xT[:, kc, :], start=(kc == 0), stop=(kc == 15)),
                    reads=[bWb, bxT], writes=[bpf])
            if oc < 4:
                isq, hd = oc < 2, oc % 2
                S.op("act", lambda e: e.activation(out=sqf[:], in_=pf[:], func=AF.Square), reads=[bpf], writes=[bsqf])
                S.op("pe", lambda e: e.matmul(psm[:], lhsT=g.ones_f[:], rhs=sqf[:], start=True, stop=True),
                     reads=[bsqf, g.bc], writes=[bpsm])
                S.op("act", lambda e: e.activation(out=rsb[:], in_=psm[:], func=AF.Sqrt, bias=EPS, scale=1.0 / 128),
                     reads=[bpsm], writes=[brsb])
                S.op("dve", lambda e: e.reciprocal(out=rsb[:], in_=rsb[:]), reads=[brsb], writes=[brsb])
                qo, bq = qko[oc % 2], bqko[oc % 2]
                gcol = gqk[:, 0:1] if isq else gqk[:, 1:2]
                S.op("dve", lambda e, qo=qo, gcol=gcol: e.scalar_tensor_tensor(
                    out=qo[:], in0=pf[:], scalar=gcol, in1=rsb[:], op0=ALU.mult, op1=ALU.mult),
                    reads=[bpf, brsb, bp], writes=[bq])
                dst = (g.qTs if isq else g.kTs)[hd][:, gi * 512:(gi + 1) * 512]
                S.dma("pool", dst, qo[:], reads=[bq], writes=[(g.b_q if isq else g.b_k)[gi]] if hd == 1 else
                      [B()] if False else [(g.b_q if isq else g.b_k)[gi]])
            else:
                ch = oc - 4
                ci, bci = cin[ch], bcin[ch]
                S.op("act", lambda e, ci=ci: e.activation(out=ci[:, 3:515], in_=pf[:], func=AF.Copy),
                     reads=[bpf], writes=[bci])
                S.op("dve", lambda e, ci=ci, ch=ch: e.tensor_scalar(
                    out=cacc[:], in0=ci[:, 0:512], scalar1=cw[:, ch, 0:1], scalar2=cb[:, ch:ch + 1],
                    op0=ALU.mult, op1=ALU.add), reads=[bci, bp], writes=[bcacc])
                for jt in range(1, 4):
                    S.op("dve", lambda e, ci=ci, ch=ch, jt=jt: e.scalar_tensor_tensor(
                        out=cacc[:], in0=ci[:, jt:jt + 512], scalar=cw[:, ch, jt:jt + 1], in1=cacc[:],
                        op0=ALU.mult, op1=ALU.add), reads=[bci, bp, bcacc], writes=[bcacc])
                if ch < 2:
                    S.op("act", lambda e, ch=ch: e.activation(out=xcf[ch][:], in_=cacc[:], func=AF.Silu),
                         reads=[bcacc], writes=[bxc])
                elif ch == 2:
                    S.op("act", lambda e: e.activation(out=Bcf[:], in_=cacc[:], func=AF.Silu), reads=[bcacc], writes=[bxc])
                    S.op("dve", lambda e: e.tensor_copy(out=Bc[:], in_=Bcf[:]), reads=[bxc], writes=[bxc])
                else:
                    S.op("act", lambda e: e.activation(out=Cc[:], in_=cacc[:], func=AF.Silu), reads=[bcacc], writes=[bxc])
                S.op("dve", lambda e, ci=ci: e.tensor_copy(out=ci[:, 0:3], in_=ci[:, 512:515]), reads=[bci], writes=[bci])
        for j in range(4):
            t = gi * 4 + j
            cs = slice(j * 128, (j + 1) * 128)
            for kc in range(16):
                S.op("pe", lambda e, kc=kc, cs=cs, xT=xT: e.matmul(
                    pv[:], lhsT=xT[:, kc, cs], rhs=Wb[:, kc, 1024:1536], start=(kc == 0), stop=(kc == 15)),
                    reads=[bWb, bxT], writes=[bpv])
            for kc in range(16):
                S.op("pe", lambda e, kc=kc, cs=cs, xT=xT: e.matmul(
                    psm[:, 0:4], lhsT=xT[:, kc, cs], rhs=Wb[:, kc, 1536:1540], start=(kc == 0), stop=(kc == 15)),
                    reads=[bWb, bxT], writes=[bpsm])
            vbt, bvt = vb[t % 2], bvb[t % 2]
            S.op("act", lambda e, vbt=vbt: e.activation(out=vbt[:], in_=pv[:, 0:256], func=AF.Copy), reads=[bpv], writes=[bvt])
            S.dma("pool", g.vs[t * 128:(t + 1) * 128, :], vbt[:], reads=[bvt], writes=[g.b_v[t]])
            S.op("act", lambda e: e.activation(out=zs[:], in_=pv[:, 256:512], func=AF.Silu), reads=[bpv], writes=[bzs])
            S.op("dve", lambda e: e.tensor_tensor(out=dt4[:], in0=psm[:, 0:4], in1=dtb, op=ALU.add),
                 reads=[bpsm, bp], writes=[bsm])
            S.op("act", lambda e: e.activation(out=dt4[:], in_=dt4[:], func=AF.Exp), reads=[bsm], writes=[bsm])
            S.op("act", lambda e: e.activation(out=dt4[:], in_=dt4[:], func=AF.Ln, bias=1.0), reads=[bsm], writes=[bsm])
            S.op("dve", lambda e: e.tensor_tensor(out=adt[:], in0=dt4[:], in1=aneg[:], op=ALU.mult),
                 reads=[bsm, bp], writes=[bsm])
            S.op("pe", lambda e: e.matmul(psm[:, 4:8], lhsT=g.tri_f[:], rhs=adt[:], start=True, stop=True),
                 reads=[bsm, g.bc], writes=[bpsm])
            S.op("act", lambda e: e.activation(out=acol[:], in_=psm[:, 4:8], func=AF.Copy), reads=[bpsm], writes=[bsm])
            S.op("dve", lambda e: e.tensor_copy(out=adt_rep[:], in_=adt[:].unsqueeze(2).to_broadcast([128, 4, 128])),
                 reads=[bsm], writes=[badr])
            for h in range(4):
                S.op("pe", lambda e, h=h: e.matmul(pacs[:, h, :], lhsT=adt_rep[:, h, :], rhs=g.tri_f[:], start=True, stop=True),
                     reads=[badr, g.bc], writes=[bpacs])
            for cc in range(2):
                S.op("pe", lambda e, cc=cc, cs=cs: e.transpose(out=pm[:, cc * 128:(cc + 1) * 128], in_=xcf[cc][:, cs],
                                                             identity=g.ident_f[:]), reads=[bxc, g.bc], writes=[bpm])
            S.op("pe", lambda e, cs=cs: e.transpose(out=pm[:, 256:384], in_=Bcf[:, cs], identity=g.ident_f[:]),
                 reads=[bxc, g.bc], writes=[bpm])
            S.op("pe", lambda e, cs=cs: e.matmul(pm[:, 384:512], lhsT=Bc[:, cs], rhs=Cc[:, cs], start=True, stop=True),
                 reads=[bxc], writes=[bpm])
            S.op("act", lambda e: e.activation(out=xtm[:], in_=pm[:, 0:256], func=AF.Copy), reads=[bpm], writes=[bxtm])
            S.op("act", lambda e: e.activation(out=Btm[:], in_=pm[:, 256:384], func=AF.Copy), reads=[bpm], writes=[bBtm])
            S.op("dve", lambda e: e.tensor_tensor(out=cbm[:], in0=pm[:, 384:512], in1=g.tri_f[:], op=ALU.mult),
                 reads=[bpm, g.bc], writes=[bcbm])
            xtm3 = xtm[:].rearrange("p (h d) -> p h d", h=4)
            S.op("dve", lambda e, xtm3=xtm3: e.tensor_tensor(
                out=xdt[:], in0=xtm3, in1=dt4[:].unsqueeze(2).to_broadcast([128, 4, 64]), op=ALU.mult),
                reads=[bxtm, bsm], writes=[bxdt])
            S.op("dve", lambda e: e.tensor_tensor(out=arg[:], in0=pacs[:], in1=acol[:].unsqueeze(2).to_broadcast([128, 4, 128]),
                                                  op=ALU.subtract), reads=[bpacs, bsm], writes=[barg])
            S.op("dve", lambda e: e.tensor_scalar(out=arg[:], in0=arg[:], scalar1=0.0, scalar2=None, op0=ALU.min),
                 reads=[barg], writes=[barg])
            S.op("act", lambda e: e.activation(out=arg[:], in_=arg[:], func=AF.Exp), reads=[barg], writes=[barg])
            S.op("dve", lambda e: e.tensor_tensor(out=Mb[:], in0=arg[:], in1=cbm[:].unsqueeze(1).to_broadcast([128, 4, 128]),
                                                  op=ALU.mult), reads=[barg, bcbm], writes=[bMb])
            S.op("act", lambda e: e.activation(out=eacs[:], in_=pacs[:], func=AF.Exp), reads=[bpacs], writes=[beacs])
            S.op("dve", lambda e, cs=cs: e.tensor_tensor(out=Cdec[:], in0=eacs[:],
                                                         in1=Cc[:, cs].unsqueeze(1).to_broadcast([128, 4, 128]), op=ALU.mult),
                 reads=[beacs, bxc], writes=[bCdec])
            for h in range(4):
                S.op("pe", lambda e, h=h: e.matmul(py[:, h * 64:(h + 1) * 64], lhsT=Mb[:, h, :], rhs=xdt[:, h, :],
                                                   start=True, stop=False), reads=[bMb, bxdt], writes=[bpy])
                S.op("pe", lambda e, h=h: e.matmul(py[:, h * 64:(h + 1) * 64], lhsT=Cdec[:, h, :], rhs=prev_b[:, h, :],
                                                   start=False, stop=True), reads=[bCdec, bprevb], writes=[bpy])
            S.op("dve", lambda e: e.tensor_tensor(out=w4[:], in0=pacs[:, :, 127], in1=acol[:], op=ALU.subtract),
                 reads=[bpacs, bsm], writes=[bsm])
            S.op("act", lambda e: e.activation(out=w4[:], in_=w4[:], func=AF.Exp), reads=[bsm], writes=[bsm])
            S.op("dve", lambda e: e.tensor_tensor(out=w4[:], in0=w4[:], in1=dt4[:], op=ALU.mult), reads=[bsm], writes=[bsm])
            S.op("dve", lambda e, xtm3=xtm3: e.tensor_tensor(
                out=xw[:], in0=xtm3, in1=w4[:].unsqueeze(2).to_broadcast([128, 4, 64]), op=ALU.mult),
                reads=[bxtm, bsm], writes=[bxw])
            S.op("pe", lambda e: e.matmul(py[:, 256:512], lhsT=Btm[:], rhs=xw[:].rearrange("p h d -> p (h d)"),
                                          start=True, stop=True), reads=[bBtm, bxw], writes=[bpy])
            S.op("dve", lambda e: e.tensor_tensor(out=yt[:], in0=xtm[:], in1=dsk[:], op=ALU.mult),
                 reads=[bxtm, bp], writes=[byt])
            S.op("dve", lambda e: e.tensor_tensor(out=yt[:], in0=py[:, 0:256], in1=yt[:], op=ALU.add),
                 reads=[bpy, byt], writes=[byt])
            yot, byot = yo[t % 2], byo[t % 2]
            S.op("dve", lambda e, yot=yot: e.tensor_tensor(out=yot[:], in0=yt[:], in1=zs[:], op=ALU.mult),
                 reads=[byt, bzs], writes=[byot])
            S.dma("pool", g.xch[t * 128:(t + 1) * 128, 256:512], yot[:], reads=[byot], writes=[g.b_xs[t]])
            for h in range(4):
                S.op("dve", lambda e, h=h: e.scalar_tensor_tensor(
                    out=prevT[:, h, :], in0=prevT[:, h, :], scalar=eacs[:, h, 127:128],
                    in1=py[:, 256 + h * 64:256 + (h + 1) * 64], op0=ALU.mult, op1=ALU.add),
                    reads=[bprev, beacs, bpy], writes=[bprev])
            S.op("act", lambda e: e.activation(out=prev_b[:], in_=prevT[:], func=AF.Copy), reads=[bprev], writes=[bprevb])


def phase_attn(g, st):
    nc, S, SEQ, NT, NG = g.nc, g.S, g.SEQ, g.NT, g.NG
    sb = lambda n, s, d: st.enter_context(nc.sbuf_tensor(n, s, d))
    ps = lambda n, s, d: st.enter_context(nc.psum_tensor(n, s, d))
    B = Buf
    kT = sb("a_kT", [128, SEQ], BF16)
    nkT = sb("a_nkT", [128, SEQ], BF16)
    vt = sb("a_vt", [128, NT, 128], BF16)
    bkT, bnkT, bvt = B(), B(), B()
    qg = [sb("a_q%d" % i, [128, 512], BF16) for i in range(2)]
    bqg = [B(), B()]
    ebuf = [sb("a_e%d" % i, [128, 512], F32) for i in range(3)]
    be = [B(), B(), B()]
    spf = [sb("a_spf%d" % i, [128, 512], F32) for i in range(3)]
    bspf = [B(), B(), B()]
    spb = [sb("a_spb%d" % i, [128, 512], BF16) for i in range(3)]
    bspb = [B(), B(), B()]
    Srun = sb("a_S", [128, 512], BF16)
    bS = B()
    tb = [sb("a_t%d" % i, [128, 512], F32) for i in range(3)]
    btb = [B(), B(), B()]
    wb = [sb("a_w%d" % i, [128, 512], BF16) for i in range(3)]
    bwb = [B(), B(), B()]
    ao = [sb("a_ao%d" % i, [128, 4, 128], F32) for i in range(2)]
    bao = [B(), B()]
    pz = [ps("a_pz%d" % i, [128, 512], F32) for i in range(3)]
    pA = [ps("a_pA%d" % i, [128, 512], F32) for i in range(3)]
    po = [ps("a_po%d" % i, [128, 4, 128], F32) for i in range(2)]
    bpz, bpA, bpo = [PB(), PB(), PB()], [PB(), PB(), PB()], [PB(), PB()]
    for h2 in range(2):
        S.dma("sp", kT[:], g.kTs[h2], reads=g.b_k, writes=[bkT])
        S.op("dve", lambda e: e.tensor_scalar(out=nkT[:], in0=kT[:], scalar1=-1.0, scalar2=None, op0=ALU.mult),
             reads=[bkT], writes=[bnkT])
        for c0 in range(0, NT, 8):
            S.dma("sp", vt[:, c0:c0 + 8, :],
                  g.vs[c0 * 128:(c0 + 8) * 128, h2 * 128:(h2 + 1) * 128].rearrange("(n p) d -> p n d", p=128),
                  reads=g.b_v[c0:c0 + 8], writes=[bvt])
        pairs = []
        for qgi in range(NG):
            for kb in range(4 * qgi + 3, -1, -1):
                pairs.append((qgi, kb))

        def stage1(idx):
            qgi, kb = pairs[idx]
            i3 = idx % 3
            qt, bq = qg[qgi % 2], bqg[qgi % 2]
            if kb == 4 * qgi + 3:
                S.dma("sp", qt[:], g.qTs[h2][:, qgi * 512:(qgi + 1) * 512], reads=[g.b_q[qgi]], writes=[bq])
            d = kb - 4 * qgi
            ks = slice(kb * 128, (kb + 1) * 128)
            S.op("pe", lambda e: e.matmul(pz[i3][:], lhsT=kT[:, ks], rhs=qt[:], start=True, stop=True),
                 reads=[bkT, bq], writes=[bpz[i3]])
            S.op("pe", lambda e: e.matmul(pA[i3][:], lhsT=nkT[:, ks], rhs=qt[:], start=True, stop=False),
                 reads=[bnkT, bq], writes=[bpA[i3]])
            S.op("act", lambda e: e.activation(out=ebuf[i3][:], in_=pz[i3][:], func=AF.Exp), reads=[bpz[i3]], writes=[be[i3]])
            S.op("act", lambda e: e.activation(out=spf[i3][:], in_=ebuf[i3][:], func=AF.Ln, bias=1.0),
                 reads=[be[i3]], writes=[bspf[i3]])
            if d >= 0:
                S.op("pool", lambda e: e.tensor_tensor(out=spb[i3][:], in0=spf[i3][:], in1=g.cm_f[:, d, :], op=ALU.mult),
                     reads=[bspf[i3], g.bc], writes=[bspb[i3]])
            else:
                S.op("pool", lambda e: e.tensor_copy(out=spb[i3][:], in_=spf[i3][:]), reads=[bspf[i3]], writes=[bspb[i3]])

        def stage2(idx):
            qgi, kb = pairs[idx]
            i3 = idx % 3
            d = kb - 4 * qgi
            first = (kb == 4 * qgi + 3)
            pot, bpot = po[qgi % 2], bpo[qgi % 2]
            S.op("pe", lambda e: e.matmul(pA[i3][:], lhsT=g.trl_b[:], rhs=spb[i3][:], start=False, stop=first),
                 reads=[bspb[i3], g.bc], writes=[bpA[i3]])
            if not first:
                S.op("pe", lambda e: e.matmul(pA[i3][:], lhsT=g.ones_b[:], rhs=Srun[:], start=False, stop=True),
                     reads=[bS, g.bc], writes=[bpA[i3]])
            S.op("dve", lambda e: e.tensor_tensor(out=tb[i3][:], in0=pA[i3][:], in1=spf[i3][:], op=ALU.add),
                 reads=[bpA[i3], bspf[i3]], writes=[btb[i3]])
            S.op("act", lambda e: e.activation(out=wb[i3][:], in_=tb[i3][:], func=AF.Exp, scale=-1.0),
                 reads=[btb[i3]], writes=[bwb[i3]])
            if d >= 0:
                S.op("pool", lambda e: e.tensor_tensor(out=wb[i3][:], in0=wb[i3][:], in1=g.cm_f[:, d, :], op=ALU.mult),
                     reads=[bwb[i3], g.bc], writes=[bwb[i3]])
            for qb in range(4):
                if d > qb:
                    continue
                S.op("pe", lambda e, qb=qb: e.matmul(
                    pot[:, qb, :], lhsT=wb[i3][:, qb * 128:(qb + 1) * 128], rhs=vt[:, kb, :],
                    start=(first and qb == 3), stop=(kb == 0), skip_group_check=True),
                    reads=[bwb[i3], bvt], writes=[bpot])
            if first:
                S.op("dve", lambda e: e.tensor_copy(out=Srun[:], in_=spb[i3][:]), reads=[bspb[i3]], writes=[bS])
            else:
                S.op("dve", lambda e: e.tensor_tensor(out=Srun[:], in0=Srun[:], in1=spb[i3][:], op=ALU.add),
                     reads=[bspb[i3], bS], writes=[bS])
            if kb == 0:
                aot, baot = ao[qgi % 2], bao[qgi % 2]
                S.op("act", lambda e: e.activation(out=aot[:], in_=pot[:], func=AF.Copy), reads=[bpot], writes=[baot])
                S.dma("pool", g.xch[qgi * 512:(qgi + 1) * 512, h2 * 128:(h2 + 1) * 128].rearrange("(b p) d -> p b d", p=128),
                      aot[:], reads=[baot], writes=g.b_xa[qgi * 4:(qgi + 1) * 4])
                if h2 == 1 and "c" in g.ph:
                    for ck in (2 * qgi, 2 * qgi + 1):
                        S.collective(lambda eng, ck=ck: eng.collective_compute(
                            "AllGather", ALU.bypass, replica_groups=[[0, 1, 2, 3], [4, 5, 6, 7]],
                            ins=[g.xch[ck * 256:(ck + 1) * 256, :].opt()], outs=[g.xin[ck * 1024:(ck + 1) * 1024, :].opt()]),
                            reads=g.b_xs[2 * ck:2 * ck + 2] + g.b_xa[2 * ck:2 * ck + 2], writes=[g.b_xin[ck]])

        stage1(0)
        for idx in range(len(pairs)):
            if idx + 1 < len(pairs):
                stage1(idx + 1)
            stage2(idx)


def load_weight_bf16(g, sb, name, src, Wb, bWb):
    S = g.S
    wst = [sb(name + "_st%d" % i, [128, D], F32) for i in range(2)]
    bw = [Buf(), Buf()]
    for kc in range(16):
        i = kc % 2
        S.dma("sp", wst[i][:], src[kc * 128:(kc + 1) * 128, :], writes=[bw[i]])
        S.op("pool" if kc % 2 else "dve", lambda e, kc=kc, i=i: e.tensor_copy(out=Wb[:, kc, :], in_=wst[i][:]),
             reads=[bw[i]], writes=[bWb])


def norm_transpose(g, src, bsrc, gain, bgain, dstT, bdst, xs, bxs, pT, bpT):
    S = g.S
    for r in range(4):
        p_, bp_ = pT[r % 2], bpT[r % 2]
        for c in range(4):
            kc = r * 4 + c
            S.op("pe", lambda e, p_=p_, c=c, kc=kc: e.transpose(out=p_[:, c, :], in_=src[:, kc * 128:(kc + 1) * 128],
                                                              identity=g.ident_f[:]), reads=[bsrc, g.bc], writes=[bp_])
        S.op("dve", lambda e, p_=p_, r=r: e.tensor_tensor(
            out=dstT[:, r * 4:(r + 1) * 4, :], in0=p_[:],
            in1=gain[:, r * 4:(r + 1) * 4].unsqueeze(2).to_broadcast([128, 4, 128]), op=ALU.mult),
            reads=[bp_, bgain], writes=[bdst])


def phase_tokens(g, st0):
    nc, S, T4, NT4 = g.nc, g.S, g.T4, g.NT4
    NT4 = min(NT4, int(os.environ.get("KTOK", "1000")))
    B = Buf
    dram = lambda n, s, d: nc.dram_tensor(n, s, d, kind="Internal").ap()
    hsc = g.y
    hnTs = dram("hnTs", [g.NT4, 128, 16 * 128], BF16)
    Gs = g.x.bitcast(BF16).rearrange("(t f) c -> t (f c)", f=4)
    b_h = [B() for _ in range(NT4)]
    b_hn = [B() for _ in range(NT4)]
    b_G = [B() for _ in range(NT4)]
    xin_v = g.xin.rearrange("(k r i) f -> k i r f", r=4, i=256)

    with ExitStack() as st:
        sb = lambda n, s, d: st.enter_context(nc.sbuf_tensor(n, s, d))
        ps = lambda n, s, d: st.enter_context(nc.psum_tensor(n, s, d))
        Wo = sb("o_Wo", [128, 16, D], BF16)
        bWo = B()
        load_weight_bf16(g, sb, "o_w", g.wo, Wo, bWo)
        gso = sb("o_gso", [128, 16], F32)
        gn2 = sb("o_gn2", [128, 16], F32)
        bp = B()
        S.dma("sp", gso[:], g.gso, writes=[bp])
        S.dma("sp", gn2[:], g.gn2, writes=[bp])
        mixraw = sb("o_mixraw", [128, 4, 512], F32)
        mixq = [sb("o_mixq%d" % i, [128, 4, 512], F32) for i in range(2)]
        bmixq = [B(), B()]
        sel = sb("o_sel", [128, 4], F32)
        S.dma("sp", sel[:], g.sel.partition_broadcast(128), writes=[bp])
        mix = sb("o_mix", [128, D], F32)
        junk = sb("o_junk", [128, D], BF16)
        ssq = sb("o_ssq", [128, 8], F32)
        mixT = sb("o_mixT", [128, 16, 128], BF16)
        xt = sb("o_xt", [128, D], F32)
        h = sb("o_h", [128, D], F32)
        hs = sb("o_hs", [128, D], F32)
        hnT = sb("o_hnT", [128, 16, 128], BF16)
        bmr, bmix, bjunk, bssq, bmixT, bxt, bh, bhs, bhnT = B(), B(), B(), B(), B(), B(), B(), B(), B()
        pT = [ps("o_pT%d" % i, [128, 4, 128], F32) for i in range(2)]
        bpT = [PB(), PB()]
        pp = [ps("o_pp%d" % i, [128, 512], F32) for i in range(4)]
        bpp = [PB() for _ in range(4)]
        for ti in range(NT4):
            ts = slice(ti * 128, (ti + 1) * 128)
            for q in range(4):
                mq, bq_ = mixq[q % 2], bmixq[q % 2]
                s0 = q * T4 + ti * 128
                S.dma("sp", mq[:], xin_v[s0 // 256][s0 % 256:s0 % 256 + 128], reads=[g.b_xin[s0 // 256]], writes=[bq_])
                if q == 0:
                    S.op("dve", lambda e, mq=mq: e.tensor_scalar(out=mixraw[:], in0=mq[:], scalar1=sel[:, 0:1], scalar2=None,
                                                                 op0=ALU.mult), reads=[bq_, bp], writes=[bmr])
                else:
                    S.op("dve", lambda e, mq=mq, q=q: e.scalar_tensor_tensor(
                        out=mixraw[:].rearrange("p r f -> p (r f)"), in0=mq[:].rearrange("p r f -> p (r f)"),
                        scalar=sel[:, q:q + 1], in1=mixraw[:].rearrange("p r f -> p (r f)"), op0=ALU.mult, op1=ALU.add),
                        reads=[bq_, bp, bmr], writes=[bmr])
            S.dma("sp", xt[:], g.xq[ts, :], writes=[bxt])
            S.op("act", lambda e: e.activation(out=mix[:, 0:1024].rearrange("p (r f) -> p r f", r=4), in_=mixraw[:, :, 0:256],
                                               func=AF.Copy), reads=[bmr], writes=[bmix])
            S.op("dve", lambda e: e.tensor_copy(out=mix[:, 1024:2048].rearrange("p (r f) -> p r f", r=4),
                                                in_=mixraw[:, :, 256:512]), reads=[bmr], writes=[bmix])
            segs = [(0, 1024), (1024, 1536), (1536, 2048)]
            for si, (a, b_) in enumerate(segs):
                S.op("act", lambda e, si=si, a=a, b_=b_: e.activation(out=junk[:, a:b_], in_=mix[:, a:b_], func=AF.Square,
                                                                     accum_out=ssq[:, si:si + 1]),
                     reads=[bmix], writes=[bjunk, bssq])
            S.op("act", lambda e: e.activation(out=ssq[:, 4:5], in_=ssq[:, 0:1], func=AF.Sqrt, bias=EPS, scale=1.0 / 1024),
                 reads=[bssq], writes=[bssq])
            S.op("act", lambda e: e.activation(out=ssq[:, 5:7], in_=ssq[:, 1:3], func=AF.Sqrt, bias=EPS, scale=1.0 / 512),
                 reads=[bssq], writes=[bssq])
            S.op("dve", lambda e: e.reciprocal(out=ssq[:, 4:7], in_=ssq[:, 4:7]), reads=[bssq], writes=[bssq])
            for si, (a, b_) in enumerate(segs):
                S.op("act", lambda e, si=si, a=a, b_=b_: e.activation(out=mix[:, a:b_], in_=mix[:, a:b_], func=AF.Copy,
                                                                     scale=ssq[:, 4 + si:5 + si]),
                     reads=[bmix, bssq], writes=[bmix])
            norm_transpose(g, mix, bmix, gso, bp, mixT, bmixT, None, None, pT, bpT)
            for n in range(4):
                for kc in range(16):
                    S.op("pe", lambda e, n=n, kc=kc: e.matmul(pp[n][:], lhsT=mixT[:, kc, :], rhs=Wo[:, kc, n * 512:(n + 1) * 512],
                                                              start=(kc == 0), stop=(kc == 15)),
                         reads=[bmixT, bWo], writes=[bpp[n]])
                S.op("dve", lambda e, n=n: e.tensor_tensor(out=h[:, n * 512:(n + 1) * 512], in0=pp[n][:],
                                                           in1=xt[:, n * 512:(n + 1) * 512], op=ALU.add),
                     reads=[bpp[n], bxt], writes=[bh])
            S.dma("pool", hsc[ts, :], h[:], reads=[bh], writes=[b_h[ti]])
            S.op("act", lambda e: e.activation(out=junk[:], in_=h[:], func=AF.Square, accum_out=ssq[:, 3:4]),
                 reads=[bh], writes=[bjunk, bssq])
            rstd_from_ssq(S, ssq[:, 3:4], ssq[:, 7:8], D, bssq, bssq)
            S.op("act", lambda e: e.activation(out=hs[:], in_=h[:], func=AF.Copy, scale=ssq[:, 7:8]),
                 reads=[bh, bssq], writes=[bhs])
            norm_transpose(g, hs, bhs, gn2, bp, hnT, bhnT, None, None, pT, bpT)
            S.dma("pool", hnTs[ti].rearrange("p (c t) -> p c t", c=16), hnT[:], reads=[bhnT], writes=[b_hn[ti]])
        if hasattr(g, "dbg_h"):
            S.dma("sp", g.dbg_h, hsc, reads=b_h, writes=[g.b_dbg])
        S.emit()

    with ExitStack() as st:
        sb = lambda n, s, d: st.enter_context(nc.sbuf_tensor(n, s, d))
        ps = lambda n, s, d: st.enter_context(nc.psum_tensor(n, s, d))
        Wq = sb("r_Wq", [128, 16, D], BF16)
        bWq = B()
        load_weight_bf16(g, sb, "r_w", g.wq, Wq, bWq)
        keysT = sb("r_keysT", [128, 16, 128], F32)
        bk = B()
        S.dma("sp", keysT[:], g.keysT.rearrange("p (c i) -> p c i", c=16), writes=[bk])
        hnT = sb("r_hnT", [128, 16, 128], BF16)
        qTf = sb("r_qTf", [128, 16, 128], F32)
        sc = sb("r_sc", [128, 8, 2, 128], F32)
        tmp = sb("r_tmp", [128, 256], F32)
        v16 = sb("r_v16", [128, 8, 2, 16], F32)
        cand = sb("r_cand", [128, 8, 16, 16], F32)
        best = sb("r_best", [128, 8, 16], F32)
        sm = sb("r_sm", [128, 8, 8], F32)
        off1 = sb("r_off1", [128, 8, 128], F32)
        thr = sb("r_thr", [128, 8, 128], F32)
        up_ = sb("r_up", [128, 4, 8, 128], F32)
        ex_ = up_
        mk_ = sb("r_mk", [128, 4, 8, 128], F32)
        Gt = sb("r_Gt", [128, 128, 128], BF16)
        Gf = sb("r_Gf", [128, 4, 128], F32)
        bhnT, bqTf, bsc, btmp, bv16, bcand, bbest, bsm = B(), B(), B(), B(), B(), B(), B(), B()
        boff, bup, bex, bmk, bGt, bGf = B(), B(), B(), B(), B(), B()
        pq = [ps("r_pq%d" % i, [128, 4, 128], F32) for i in range(4)]
        bpq = [PB() for _ in range(4)]
        psc = [ps("r_psc%d" % i, [128, 4, 128], F32) for i in range(4)]
        bpsc = [PB() for _ in range(4)]
        for ti in range(NT4):
            S.dma("sp", hnT[:], hnTs[ti].rearrange("p (c t) -> p c t", c=16), reads=[b_hn[ti]], writes=[bhnT])
            for hc in range(16):
                for kc in range(16):
                    S.op("pe", lambda e, hc=hc, kc=kc: e.matmul(pq[hc // 4][:, hc % 4, :], lhsT=Wq[:, kc, hc * 128:(hc + 1) * 128],
                                                                rhs=hnT[:, kc, :], start=(kc == 0), stop=(kc == 15)),
                         reads=[bWq, bhnT], writes=[bpq[hc // 4]])
                if hc % 4 == 3:
                    S.op("act", lambda e, hc=hc: e.activation(out=qTf[:, hc - 3:hc + 1, :], in_=pq[hc // 4][:], func=AF.Copy),
                         reads=[bpq[hc // 4]], writes=[bqTf])
            scv = sc[:].rearrange("p h c i -> p (h c) i")
            for hc in range(16):
                S.op("pe", lambda e, hc=hc: e.matmul(psc[hc // 4][:, hc % 4, :], lhsT=qTf[:, hc, :], rhs=keysT[:, hc, :],
                                                     start=True, stop=True), reads=[bqTf, bk], writes=[bpsc[hc // 4]])
                if hc % 4 == 3:
                    S.op("act", lambda e, hc=hc: e.activation(out=scv[:, hc - 3:hc + 1, :], in_=psc[hc // 4][:], func=AF.Copy),
                         reads=[bpsc[hc // 4]], writes=[bsc])
            v16v = v16[:].rearrange("p h c k -> p (h c) k")
            for hc in range(16):
                S.op("dve", lambda e, hc=hc: e.max(out=v16v[:, hc, 0:8], in_=scv[:, hc, :]), reads=[bsc], writes=[bv16])
                S.op("dve", lambda e, hc=hc: e.match_replace(out=tmp[:, 0:128], in_to_replace=v16v[:, hc, 0:8],
                                                             in_values=scv[:, hc, :], imm_value=NEG),
                     reads=[bsc, bv16], writes=[btmp])
                S.op("dve", lambda e, hc=hc: e.max(out=v16v[:, hc, 8:16], in_=tmp[:, 0:128]), reads=[btmp], writes=[bv16])
            S.op("dve", lambda e: e.tensor_tensor(out=cand[:], in0=v16[:, :, 0, :].unsqueeze(3).to_broadcast([128, 8, 16, 16]),
                                                  in1=v16[:, :, 1, :].unsqueeze(2).to_broadcast([128, 8, 16, 16]), op=ALU.add),
                 reads=[bv16], writes=[bcand])
            for hh in range(8):
                cv = cand[:, hh].rearrange("p a b -> p (a b)")
                S.op("dve", lambda e, hh=hh, cv=cv: e.max(out=best[:, hh, 0:8], in_=cv), reads=[bcand], writes=[bbest])
                S.op("dve", lambda e, hh=hh, cv=cv: e.match_replace(out=tmp[:], in_to_replace=best[:, hh, 0:8], in_values=cv,
                                                                    imm_value=NEG), reads=[bcand, bbest], writes=[btmp])
                S.op("dve", lambda e, hh=hh: e.max(out=best[:, hh, 8:16], in_=tmp[:]), reads=[btmp], writes=[bbest])
            S.op("dve", lambda e: e.tensor_tensor(out=cand[:, :, 0, :], in0=best[:], in1=best[:, :, 0:1].to_broadcast([128, 8, 16]),
                                                  op=ALU.subtract), reads=[bbest, bcand], writes=[bcand])
            S.op("act", lambda e: e.activation(out=cand[:, :, 0, :], in_=cand[:, :, 0, :], func=AF.Exp), reads=[bcand], writes=[bcand])
            S.op("dve", lambda e: e.tensor_reduce(out=sm[:, :, 0], in_=cand[:, :, 0, :], axis=AX.X, op=ALU.add),
                 reads=[bcand], writes=[bsm])
            S.op("act", lambda e: e.activation(out=sm[:, :, 1], in_=sm[:, :, 0], func=AF.Ln), reads=[bsm], writes=[bsm])
            S.op("dve", lambda e: e.tensor_tensor(out=sm[:, :, 2], in0=sm[:, :, 1], in1=best[:, :, 0], op=ALU.add),
                 reads=[bsm, bbest], writes=[bsm])
            S.op("dve", lambda e: e.tensor_tensor(out=sm[:, :, 3], in0=best[:, :, 15], in1=sm[:, :, 2], op=ALU.subtract),
                 reads=[bsm, bbest], writes=[bsm])
            S.op("act", lambda e: e.activation(out=sm[:, :, 4], in_=sm[:, :, 3], func=AF.Exp), reads=[bsm], writes=[bsm])
            S.op("dve", lambda e: e.tensor_tensor(out=thr[:], in0=best[:, :, 15:16].to_broadcast([128, 8, 128]), in1=sc[:, :, 0, :],
                                                  op=ALU.subtract), reads=[bsc, bbest], writes=[boff])
            s2b = sc[:, :, 1, :].unsqueeze(1).to_broadcast([128, 4, 8, 128])
            for ib in range(32):
                isl = slice(ib * 4, ib * 4 + 4)
                t1 = thr[:, :, isl].rearrange("p h i -> p i h").unsqueeze(3).to_broadcast([128, 4, 8, 128])
                S.op("pool", lambda e, t1=t1: e.tensor_tensor(out=up_[:], in0=s2b, in1=t1, op=ALU.subtract),
                     reads=[bsc, boff], writes=[bup])
                S.op("act", lambda e: e.activation(out=mk_[:], in_=up_[:], func=AF.Exp), reads=[bup], writes=[bmk])
                S.op("dve", lambda e: e.scalar_tensor_tensor(
                    out=mk_[:].rearrange("p i h j -> p (i h j)"), in0=up_[:].rearrange("p i h j -> p (i h j)"), scalar=0.0,
                    in1=mk_[:].rearrange("p i h j -> p (i h j)"), op0=ALU.is_ge, op1=ALU.mult), reads=[bup, bmk], writes=[bmk])
                S.op("dve", lambda e: e.tensor_scalar(out=Gf[:], in0=mk_[:, :, 0, :], scalar1=sm[:, 0, 4:5], scalar2=None,
                                                      op0=ALU.mult), reads=[bmk, bsm], writes=[bGf])
                for hh in range(1, 8):
                    if hh < 7:
                        S.op("dve", lambda e, hh=hh: e.scalar_tensor_tensor(
                            out=Gf[:], in0=mk_[:, :, hh, :], scalar=sm[:, hh, 4:5], in1=Gf[:], op0=ALU.mult, op1=ALU.add),
                            reads=[bmk, bsm, bGf], writes=[bGf])
                    else:
                        S.op("dve", lambda e, hh=hh, isl=isl: e.scalar_tensor_tensor(
                            out=Gt[:, isl, :], in0=mk_[:, :, hh, :], scalar=sm[:, hh, 4:5], in1=Gf[:], op0=ALU.mult, op1=ALU.add),
                            reads=[bmk, bsm, bGf], writes=[bGt])
            S.dma("pool", Gs[ti * 128:(ti + 1) * 128, :], Gt[:].rearrange("p a b -> p (a b)"), reads=[bGt], writes=[b_G[ti]])
        if hasattr(g, "dbg_G"):
            pass
        S.emit()

    with ExitStack() as st:
        sb = lambda n, s, d: st.enter_context(nc.sbuf_tensor(n, s, d))
        ps = lambda n, s, d: st.enter_context(nc.psum_tensor(n, s, d))
        NTG = min(2, NT4)
        hn2 = sb("f_hn", [128, NTG, 16, 128], BF16)
        bhn2 = B()
        HT = sb("f_HT", [128, 128, NTG, 128], BF16)
        bHT = B()
        dT = [sb("f_dT%d" % i, [128, 16, 512], BF16) for i in range(2)]
        bdT = [B(), B()]
        Gc = [sb("f_Gc%d" % i, [128, NTG, 512], BF16) for i in range(2)]
        bGc = [B(), B()]
        ge = [sb("f_ge%d" % i, [128, 512], F32) for i in range(2)]
        bge = [B(), B()]
        Hb = [sb("f_Hb%d" % i, [128, 512], BF16) for i in range(2)]
        bHb = [B(), B()]
        upb = [sb("f_up%d" % i, [128, D], BF16) for i in range(4)]
        bupb = [B() for _ in range(4)]
        hres = sb("f_hres", [128, D], F32)
        bhres = B()
        outt = [sb("f_out%d" % i, [128, D], F32) for i in range(2)]
        bout = [B(), B()]
        bank = [ps("f_bk%d" % i, [128, 512], F32) for i in range(8)]
        bbank = [PB() for _ in range(8)]
        pTb = [bank[4][:].bitcast(BF16), bank[5][:].bitcast(BF16)]
        k = 0
        for gi in range(0, NT4, NTG):
            for j in range(NTG):
                S.dma("sp", hn2[:, j], hnTs[gi + j].rearrange("p (c t) -> p c t", c=16), reads=[b_hn[gi + j]], writes=[bhn2])
            for e4 in range(NE // 512):
                i = e4 % 2
                S.dma("sp", dT[i][:], g.dTs[e4].rearrange("p (c e) -> p c e", c=16), reads=[g.b_dTs[e4]], writes=[bdT[i]])
                S.dma("sp", Gc[i][:], Gs[gi * 128:(gi + NTG) * 128, e4 * 512:(e4 + 1) * 512].rearrange("(j p) e -> p j e", p=128),
                      reads=b_G[gi:gi + NTG], writes=[bGc[i]])
                for j in range(NTG):
                    a = k % 2
                    k += 1
                    for kc in range(16):
                        S.op("pe", lambda e, a=a, j=j, kc=kc, i=i: e.matmul(bank[a][:], lhsT=hn2[:, j, kc, :], rhs=dT[i][:, kc, :],
                                                                            start=(kc == 0), stop=(kc == 15)),
                             reads=[bhn2, bdT[i]], writes=[bbank[a]])
                    S.op("act", lambda e, a=a: e.activation(out=ge[a][:], in_=bank[a][:], func=AF.Gelu),
                         reads=[bbank[a]], writes=[bge[a]])
                    S.op("dve", lambda e, a=a, i=i, j=j: e.tensor_tensor(out=Hb[a][:], in0=ge[a][:], in1=Gc[i][:, j, :], op=ALU.mult),
                         reads=[bge[a], bGc[i]], writes=[bHb[a]])
                    for c in range(4):
                        S.op("pe", lambda e, a=a, c=c: e.transpose(out=pTb[a][:, c * 128:(c + 1) * 128],
                                                                   in_=Hb[a][:, c * 128:(c + 1) * 128], identity=g.ident_b[:]),
                             reads=[bHb[a], g.bc], writes=[bbank[4 + a]])
                    S.op("act", lambda e, a=a, e4=e4, j=j: e.activation(
                        out=HT[:, e4 * 4:(e4 + 1) * 4, j, :], in_=pTb[a][:, 0:512].rearrange("p (c t) -> p c t", c=4), func=AF.Copy),
                        reads=[bbank[4 + a]], writes=[bHT])
            for eb in range(NE // 128):
                u = eb % 4
                S.dma("pool", upb[u][:], g.ups[eb], reads=[g.b_ups[eb]], writes=[bupb[u]])
                for j in range(NTG):
                    for n in range(4):
                        S.op("pe", lambda e, j=j, n=n, eb=eb, u=u: e.matmul(bank[j * 4 + n][:], lhsT=HT[:, eb, j, :],
                                                                            rhs=upb[u][:, n * 512:(n + 1) * 512],
                                                                            start=(eb == 0), stop=(eb == NE // 128 - 1)),
                             reads=[bHT, bupb[u]], writes=[bbank[j * 4 + n]])
            for j in range(NTG):
                ts = slice((gi + j) * 128, (gi + j + 1) * 128)
                S.dma("sp", hres[:], hsc[ts, :], reads=[b_h[gi + j]], writes=[bhres])
                for n in range(4):
                    S.op("dve", lambda e, j=j, n=n: e.tensor_tensor(out=outt[j][:, n * 512:(n + 1) * 512], in0=bank[j * 4 + n][:],
                                                                   in1=hres[:, n * 512:(n + 1) * 512], op=ALU.add),
                         reads=[bbank[j * 4 + n], bhres], writes=[bout[j]])
                S.dma("pool", g.y[ts, :], outt[j][:], reads=[bout[j]], writes=[g.b_y, b_h[gi + j]])


def kernel(x, attn_norm, w_in, sb_q_norm, sb_k_norm, conv_w, conv_b, dt_bias, a_log, d_skip,
           sb_out_norm, ssd_out_norm, w_out, ffn_norm, peer_query, peer_sub_keys, peer_down, peer_up,
           _debug=False):
    x = np.asarray(x, np.float32)
    Bsz, SEQ, _ = x.shape
    assert Bsz == 2 and SEQ % 512 == 0
    T4 = SEQ // 4
    f = lambda a: np.ascontiguousarray(np.asarray(a, np.float32))
    w_in0 = np.asarray(w_in, np.float32)[0]
    pc = lambda v: f(np.asarray(v, np.float32).reshape(16, 128).T)
    conv_w0 = np.asarray(conv_w, np.float32)[0, :, 0, :]
    conv_b0 = np.asarray(conv_b, np.float32)[0]
    keysT = f(np.asarray(peer_sub_keys, np.float32)[0].reshape(16, 128, 128).transpose(2, 0, 1).reshape(128, 16 * 128))
    gso = pc(np.concatenate([np.asarray(sb_out_norm, np.float32)[0], np.asarray(ssd_out_norm, np.float32)[0]]))
    shared = {
        "gn1": pc(attn_norm[0]), "gso": gso, "wo": f(w_out[0]), "gn2": pc(ffn_norm[0]), "wq": f(peer_query[0]),
        "keysT": keysT,
        "down": f(peer_down[0]) if "t" in os.environ.get("KPH", "tpacx") else f(peer_down[0][:128]),
        "up": f(peer_up[0]) if "t" in os.environ.get("KPH", "tpacx") else f(peer_up[0][:128]),
        "gqk": f(np.stack([np.asarray(sb_q_norm, np.float32)[0], np.asarray(sb_k_norm, np.float32)[0]], axis=1)),
    }
    in_maps = []
    for c in range(8):
        b, hq = c // 4, c % 4
        grp = hq // 2
        cols = np.concatenate([
            np.arange(256 * hq, 256 * hq + 256),
            1024 + np.arange(256 * hq, 256 * hq + 256),
            4096 + np.arange(256 * hq, 256 * hq + 256),
            4096 + 1024 + np.arange(128 * grp, 128 * grp + 128),
            4096 + 1024 + 256 + np.arange(128 * grp, 128 * grp + 128),
            2048 + np.arange(256 * hq, 256 * hq + 256),
            3072 + np.arange(256 * hq, 256 * hq + 256),
            4096 + 1536 + np.arange(4 * hq, 4 * hq + 4),
        ])
        cch = np.concatenate([np.arange(256 * hq, 256 * hq + 256), 1024 + np.arange(128 * grp, 128 * grp + 128),
                              1024 + 256 + np.arange(128 * grp, 128 * grp + 128)])
        cwc = conv_w0[:, cch]
        ssdp = np.zeros((1, 12), np.float32)
        ssdp[0, 0:4] = np.asarray(dt_bias, np.float32)[0, 4 * hq:4 * hq + 4]
        ssdp[0, 4:8] = np.asarray(a_log, np.float32)[0, 4 * hq:4 * hq + 4]
        m = dict(shared)
        m.update({
            "x": f(x[b]), "xq": f(x[b, hq * T4:(hq + 1) * T4]), "w1": f(w_in0[:, cols]),
            "cw": f(cwc.reshape(4, 4, 128).transpose(2, 1, 0).reshape(128, 16)),
            "cb": f(conv_b0[cch].reshape(4, 128).T), "ssdp": ssdp,
            "sel": f(np.eye(4, dtype=np.float32)[hq][None, :]),
            "dsk": f(np.repeat(np.asarray(d_skip, np.float32)[0, 4 * hq:4 * hq + 4], 64)[None, :]),
        })
        in_maps.append(m)
    nc = build_program(SEQ, debug=_debug)
    res = run_bass_kernel_spmd(nc, in_maps, core_ids=list(range(8)))
    out = np.empty((2, SEQ, D), np.float32)
    for c in range(8):
        b, hq = c // 4, c % 4
        out[b, hq * T4:(hq + 1) * T4] = res.results[c]["y"]
    if _debug:
        return out, res
    return out
```
